# Optimizing a Trainium2 kernel written in Bass

```python
import jax, jax.numpy as jnp
from jax import lax
import numpy as np

D_MODEL = 1024
BATCH = 2
SEQ = 8192
DEPTH = 1
DEC_BATCH = 32
DEC_SEQ = 64
PAST_LEN = 4096

CHUNK = 64
SUB = 16
N_MEM = 256
EPS = 1e-6
ROPE_BASE = 10000.0
HG_H = 8
HG_DK = 128
HG_DV = 128
HG_KW = HG_H * HG_DK
HG_W = HG_H * HG_DV
RT_H = 4
RT_DK = 64
RT_DV = 128
RT_QK = RT_H * RT_DK
RT_W = RT_H * RT_DV
XA_H = 4
XA_DH = 128
XA_W = XA_H * XA_DH
D_MIX = HG_W + RT_W + XA_W
SPLIT_SIZES = (HG_KW, HG_KW, HG_W, HG_W, RT_QK, RT_QK, RT_W, RT_W, XA_W, XA_W)
D_IN = 2 * HG_KW + 2 * HG_W + 2 * RT_QK + 2 * RT_W + 2 * XA_W

kernel_name = "hgrn2_retention_memxattn_parallel_stream_step"


def _split_points():
    pts, acc = [], 0
    for s in SPLIT_SIZES[:-1]:
        acc += s
        pts.append(acc)
    return pts


def rms_norm(x, g):
    x32 = x.astype(jnp.float32)
    r = x32 * lax.rsqrt(jnp.mean(x32 * x32, axis=-1, keepdims=True) + EPS)
    return r * g.astype(jnp.float32)


def head_rms_norm(o, g):
    h, d = o.shape[-2], o.shape[-1]
    r = o * lax.rsqrt(jnp.mean(o * o, axis=-1, keepdims=True) + EPS)
    return r * g.astype(jnp.float32).reshape(h, d)


def head_group_norm(o, g):
    h, d = o.shape[-2], o.shape[-1]
    mu = jnp.mean(o, axis=-1, keepdims=True)
    c = o - mu
    r = c * lax.rsqrt(jnp.mean(c * c, axis=-1, keepdims=True) + EPS)
    return r * g.astype(jnp.float32).reshape(h, d)


def apply_rope(x, pos):
    half = x.shape[-1] // 2
    inv_freq = ROPE_BASE ** (-jnp.arange(half, dtype=jnp.float32) / half)
    ang = pos.astype(jnp.float32)[:, None] * inv_freq[None, :]
    cos, sin = jnp.cos(ang)[None, :, None, :], jnp.sin(ang)[None, :, None, :]
    x1, x2 = x[..., :half], x[..., half:]
    return jnp.concatenate([x1 * cos - x2 * sin, x1 * sin + x2 * cos], axis=-1)


def hgrn2_chunkwise(q, k, v, g, s0):
    b, t, h, dk = q.shape
    dv = v.shape[-1]
    nc, nsb = t // CHUNK, CHUNK // SUB

    def blk(a):
        return a.reshape(b, nc, nsb, SUB, h, a.shape[-1]).transpose(0, 4, 1, 2, 3, 5)

    q, k, v, g = blk(q), blk(k), blk(v), blk(g)
    gc = jnp.cumsum(g.reshape(b, h, nc, CHUNK, dk), axis=3).reshape(b, h, nc, nsb, SUB, dk)
    g_end = gc[..., -1, :]
    g_ref = jnp.concatenate([jnp.zeros_like(g_end[:, :, :, :1]), g_end[:, :, :, :-1]], axis=3)
    g_chunk = g_end[:, :, :, -1]
    q_ref = q * jnp.exp(gc - g_ref[..., None, :])
    k_ref = k * jnp.exp(g_ref[..., None, :] - gc)
    causal = jnp.tril(jnp.ones((SUB, SUB), dtype=bool))
    a_diag = jnp.where(causal, jnp.einsum('bhcaid,bhcajd->bhcaij', q_ref, k_ref), 0.0)
    o = jnp.einsum('bhcaij,bhcajv->bhcaiv', a_diag, v)
    k_end = k * jnp.exp(g_end[..., None, :] - gc)
    sb_lower = jnp.arange(nsb)[:, None] > jnp.arange(nsb)[None, :]
    mid = jnp.exp(jnp.where(sb_lower[..., None],
                            g_ref[:, :, :, :, None, :] - g_end[:, :, :, None, :, :], -jnp.inf))
    a_off = jnp.einsum('bhcaid,bhcaed,bhcejd->bhcaeij', q_ref, mid, k_end)
    o = o + jnp.einsum('bhcaeij,bhcejv->bhcaiv', a_off, v)
    k_chunk = k * jnp.exp(g_chunk[:, :, :, None, None, :] - gc)
    ds = jnp.einsum('bhcaid,bhcaiv->bhcdv', k_chunk, v)

    def step(s, inp):
        decay, d = inp
        return decay[..., None] * s + d, s

    s_fin, s_starts = lax.scan(step, s0, (jnp.moveaxis(jnp.exp(g_chunk), 2, 0), jnp.moveaxis(ds, 2, 0)))
    s_starts = jnp.moveaxis(s_starts, 0, 2)
    o = o + jnp.einsum('bhcaid,bhcdv->bhcaiv', q * jnp.exp(gc), s_starts)
    o = o.transpose(0, 2, 3, 4, 1, 5).reshape(b, t, h, dv)
    return o, s_fin


def retention_chunkwise(q, k, v, r0, chunk):
    b, t, h, dk = q.shape
    dv = v.shape[-1]
    nc = t // chunk
    log_gamma = jnp.log1p(-jnp.exp2(-5.0 - jnp.arange(h, dtype=jnp.float32)))
    pos = jnp.arange(chunk, dtype=jnp.float32)
    lg = log_gamma[:, None]
    intra = jnp.exp(lg[..., None] * jnp.abs(pos[:, None] - pos[None, :]))
    q_decay = jnp.exp(lg * (pos + 1.0))
    k_decay = jnp.exp(lg * (chunk - 1.0 - pos))
    chunk_decay = jnp.exp(log_gamma * chunk)
    qc = q.reshape(b, nc, chunk, h, dk)
    kc = k.reshape(b, nc, chunk, h, dk)
    vc = v.reshape(b, nc, chunk, h, dv)
    scores = jnp.einsum('bcihd,bcjhd->bchij', qc, kc) * intra
    o = jnp.einsum('bchij,bcjhv->bcihv', scores, vc)
    dr = jnp.einsum('bcjhd,hj,bcjhv->bchdv', kc, k_decay, vc)

    def step(r, d):
        return chunk_decay[:, None, None] * r + d, r

    r_fin, r_starts = lax.scan(step, r0, jnp.moveaxis(dr, 1, 0))
    r_starts = jnp.moveaxis(r_starts, 0, 1)
    o = o + jnp.einsum('bcihd,hi,bchdv->bcihv', qc, q_decay, r_starts)
    return o.reshape(b, t, h, dv), r_fin


def memory_kv(mem, mem_norm_g, w_mem_k, w_mem_v):
    m = rms_norm(mem, mem_norm_g)
    b = mem.shape[0]
    mk = (m @ w_mem_k.astype(jnp.float32)).reshape(b, N_MEM, XA_H, XA_DH)
    mv = (m @ w_mem_v.astype(jnp.float32)).reshape(b, N_MEM, XA_H, XA_DH)
    return mk, mv


def mixer_layer(x, pos, s_hg, s_rt, mem_k, mem_v, norm_g, w_in, lb, hg_norm_g, rt_norm_g, w_out):
    b, t, _ = x.shape
    f32 = jnp.float32
    h = rms_norm(x, norm_g)
    proj = h @ w_in.astype(f32)
    hg_q, hg_f, hg_i, hg_gate, rt_q, rt_k, rt_v, rt_gate, xa_q, xa_gate = jnp.split(proj, _split_points(), axis=-1)

    lb = lb.astype(f32)
    sig = jax.nn.sigmoid(hg_f)
    log_f = jnp.log(lb + (1.0 - lb) * sig)
    k_in = (1.0 - lb) * jax.nn.sigmoid(-hg_f)
    qa = jax.nn.silu(hg_q).reshape(b, t, HG_H, HG_DK)
    ka = k_in.reshape(b, t, HG_H, HG_DK)
    ga = log_f.reshape(b, t, HG_H, HG_DK)
    va = hg_i.reshape(b, t, HG_H, HG_DV)
    pad = (-t) % CHUNK
    pw = ((0, 0), (0, pad), (0, 0), (0, 0))
    o_hg, s_hg_new = hgrn2_chunkwise(jnp.pad(qa, pw), jnp.pad(ka, pw), jnp.pad(va, pw), jnp.pad(ga, pw),
                                     s_hg.astype(f32))
    o_hg = head_rms_norm(o_hg[:, :t], hg_norm_g).reshape(b, t, HG_W) * jax.nn.silu(hg_gate)

    rq = apply_rope(rt_q.reshape(b, t, RT_H, RT_DK), pos)
    rk = apply_rope(rt_k.reshape(b, t, RT_H, RT_DK), pos) * (RT_DK ** -0.5)
    rv = rt_v.reshape(b, t, RT_H, RT_DV)
    o_rt, s_rt_new = retention_chunkwise(rq, rk, rv, s_rt.astype(f32), min(CHUNK, t))
    o_rt = head_group_norm(o_rt, rt_norm_g).reshape(b, t, RT_W) * jax.nn.silu(rt_gate)

    xq = xa_q.reshape(b, t, XA_H, XA_DH) * (XA_DH ** -0.5)
    att = jax.nn.softmax(jnp.einsum('bthd,bmhd->bhtm', xq, mem_k.astype(f32)), axis=-1)
    o_xa = jnp.einsum('bhtm,bmhd->bthd', att, mem_v.astype(f32)).reshape(b, t, XA_W) * jax.nn.silu(xa_gate)

    mixed = jnp.concatenate([o_hg, o_rt, o_xa], axis=-1)
    out = x.astype(f32) + mixed @ w_out.astype(f32)
    return out, s_hg_new, s_rt_new


def setup_inputs(seed: int = 0) -> dict:
    key = jax.random.key(seed)
    ks = jax.random.split(key, 18)
    f32 = jnp.float32

    def nrm(k, shape, scale=1.0):
        return scale * jax.random.normal(k, shape, f32)

    return {
        "x_prompt": nrm(ks[0], (BATCH, SEQ, D_MODEL)),
        "x_sample": nrm(ks[1], (DEC_BATCH, DEC_SEQ, D_MODEL)),
        "mem_prompt": nrm(ks[2], (BATCH, N_MEM, D_MODEL)),
        "state_hgrn": nrm(ks[3], (DEPTH, DEC_BATCH, HG_H, HG_DK, HG_DV)),
        "state_ret": nrm(ks[4], (DEPTH, DEC_BATCH, RT_H, RT_DK, RT_DV), 4.0),
        "cache_mem_k": nrm(ks[5], (DEPTH, DEC_BATCH, N_MEM, XA_H, XA_DH)),
        "cache_mem_v": nrm(ks[6], (DEPTH, DEC_BATCH, N_MEM, XA_H, XA_DH)),
        "norm_g": 1.0 + nrm(ks[7], (DEPTH, D_MODEL), 0.02),
        "w_in": nrm(ks[8], (DEPTH, D_MODEL, D_IN), D_MODEL ** -0.5),
        "lb_logits": nrm(ks[9], (DEPTH + 1, HG_KW), 0.1),
        "hg_norm_g": 1.0 + nrm(ks[10], (DEPTH, HG_W), 0.02),
        "rt_norm_g": 1.0 + nrm(ks[11], (DEPTH, RT_W), 0.02),
        "mem_norm_g": 1.0 + nrm(ks[12], (DEPTH, D_MODEL), 0.02),
        "w_mem_k": nrm(ks[13], (DEPTH, D_MODEL, XA_W), D_MODEL ** -0.5),
        "w_mem_v": nrm(ks[14], (DEPTH, D_MODEL, XA_W), D_MODEL ** -0.5),
        "w_out": nrm(ks[15], (DEPTH, D_MIX, D_MODEL), D_MIX ** -0.5),
        "final_norm_g": 1.0 + nrm(ks[16], (D_MODEL,), 0.02),
    }


def reference(x_prompt, x_sample, mem_prompt, state_hgrn, state_ret, cache_mem_k, cache_mem_v,
              norm_g, w_in, lb_logits, hg_norm_g, rt_norm_g, mem_norm_g, w_mem_k, w_mem_v, w_out,
              final_norm_g):
    f32 = jnp.float32
    bp, tp, _ = x_prompt.shape
    ts = x_sample.shape[1]
    pos_p = jnp.arange(tp)
    pos_s = PAST_LEN + jnp.arange(ts)
    lb_all = jnp.cumsum(jax.nn.softmax(lb_logits.astype(f32), axis=0), axis=0)
    hp, hs = x_prompt, x_sample
    hg_p, rt_p, mk_p, mv_p, hg_s, rt_s = [], [], [], [], [], []
    for l in range(DEPTH):
        mk, mv = memory_kv(mem_prompt, mem_norm_g[l], w_mem_k[l], w_mem_v[l])
        hp, shg, srt = mixer_layer(hp, pos_p, jnp.zeros((bp, HG_H, HG_DK, HG_DV), f32),
                                   jnp.zeros((bp, RT_H, RT_DK, RT_DV), f32), mk, mv,
                                   norm_g[l], w_in[l], lb_all[l], hg_norm_g[l], rt_norm_g[l], w_out[l])
        hg_p.append(shg); rt_p.append(srt); mk_p.append(mk); mv_p.append(mv)
        hs, shg2, srt2 = mixer_layer(hs, pos_s, state_hgrn[l], state_ret[l], cache_mem_k[l], cache_mem_v[l],
                                     norm_g[l], w_in[l], lb_all[l], hg_norm_g[l], rt_norm_g[l], w_out[l])
        hg_s.append(shg2); rt_s.append(srt2)
    dt = x_prompt.dtype
    y_prompt = rms_norm(hp, final_norm_g).astype(dt)
    y_sample = rms_norm(hs, final_norm_g).astype(x_sample.dtype)
    new_state_hgrn_prompt = jnp.stack(hg_p).astype(dt)
    new_state_ret_prompt = jnp.stack(rt_p).astype(dt)
    new_mem_k_prompt = jnp.stack(mk_p).astype(dt)
    new_mem_v_prompt = jnp.stack(mv_p).astype(dt)
    new_state_hgrn_sample = jnp.stack(hg_s).astype(state_hgrn.dtype)
    new_state_ret_sample = jnp.stack(rt_s).astype(state_ret.dtype)
    return (y_prompt, y_sample, new_state_hgrn_prompt, new_state_ret_prompt, new_mem_k_prompt, new_mem_v_prompt,
            new_state_hgrn_sample, new_state_ret_sample)
```

```python
import math
from contextlib import ExitStack

import numpy as np
import concourse.bass as bass
import concourse.mybir as mybir
from concourse.bass_utils import run_bass_kernel_spmd

F32 = mybir.dt.float32
BF16 = mybir.dt.bfloat16
I32 = mybir.dt.int32
AF = mybir.ActivationFunctionType
ALU = mybir.AluOpType
AX = mybir.AxisListType

D = 1024
DIN = 6656
DMIX = 2048
NMEM = 256
C_Q, C_F, C_I, C_G, C_RQ, C_RK, C_RV, C_RG, C_XQ, C_XG = 0, 1024, 2048, 3072, 4096, 4352, 4608, 5120, 5632, 6144
EPS = 1e-6
PAST_LEN = 4096
LG = [math.log1p(-(2.0 ** (-5 - h))) for h in range(4)]
NCORES = 8
NDS = 12
TWO_PI = 2.0 * math.pi
CW1 = 6.28125
CW2 = TWO_PI - CW1


class Sem:
    def __init__(self, h, step):
        self.h = h
        self.step = step
        self.count = 0


class Buf:
    __slots__ = ("w", "r")

    def __init__(self):
        self.w = None
        self.r = {}


class Prog:
    ENG = ("pe", "act", "dve", "pool", "sp")

    def __init__(self, nc, es):
        self.nc = nc
        self.q = {k: [] for k in self.ENG}
        self.esem = {k: Sem(es.enter_context(nc.semaphore("sem_" + k)), 1) for k in ("pe", "act", "dve", "pool")}
        self.dsems = [Sem(es.enter_context(nc.semaphore("dsem%d" % i)), 16) for i in range(NDS)]
        self.di = 0
        self.waited = {k: {} for k in self.ENG}
        self.nops = 0

    def op(self, eng, fn, reads=(), writes=()):
        deps = {}

        def add(sem, c):
            if deps.get(sem, 0) < c:
                deps[sem] = c

        for b in reads:
            if b.w is not None:
                add(*b.w)
        for b in writes:
            if b.w is not None:
                add(*b.w)
            for sem, c in b.r.items():
                add(sem, c)
        if eng == "sp":
            mysem = self.dsems[self.di % len(self.dsems)]
            self.di += 1
            if mysem.count > 0:
                add(mysem, mysem.count)
        else:
            mysem = self.esem[eng]
        w = self.waited[eng]
        for sem, c in deps.items():
            if eng == "pe" and sem is self.esem["pe"]:
                continue
            if w.get(sem, 0) >= c:
                continue
            w[sem] = c
            self.q[eng].append(("w", sem.h, c))
        mysem.count += mysem.step
        c = mysem.count
        self.q[eng].append(("i", fn, mysem.h, mysem.step))
        self.nops += 1
        for b in reads:
            if b.r.get(mysem, 0) < c:
                b.r[mysem] = c
        for b in writes:
            b.w = (mysem, c)
            b.r = {}

    def finish(self):
        for sem in self.dsems:
            if sem.count > 0:
                self.q["sp"].append(("w", sem.h, sem.count))

    def emit(self, block):
        def mk(name):
            def f(e):
                for it in self.q[name]:
                    if it[0] == "w":
                        e.wait_ge(it[1], it[2])
                    else:
                        it[1](e).then_inc(it[2], it[3])
            return f

        block.tensor(mk("pe"))
        block.scalar(mk("act"))
        block.vector(mk("dve"))
        block.gpsimd(mk("pool"))
        block.sync(mk("sp"))


def build(NPRE, NMAIN, NSAMP):
    NTRAV = NPRE + NMAIN
    nc = bass.Bass("TRN2", target_bir_lowering=False)

    def din(name, shape):
        return nc.dram_tensor(name, list(shape), F32, kind="ExternalInput").ap()

    def dout(name, shape):
        return nc.dram_tensor(name, list(shape), F32, kind="ExternalOutput").ap()

    xp_d = din("xp", [NTRAV * 128, D])
    xs_d = din("xs", [NSAMP * 64, D])
    mem_d = din("mem", [NMEM, D])
    sthg_d = din("sthg", [NSAMP, 8, 128, 128])
    strt_d = din("strt", [NSAMP, 4, 64, 128])
    ck_d = din("ck", [NSAMP, NMEM, 512])
    cv_d = din("cv", [NSAMP, NMEM, 512])
    win_d = din("win", [D, DIN])
    wout_d = din("wout", [DMIX, D])
    wmk_d = din("wmk", [D, 512])
    wmv_d = din("wmv", [D, 512])
    ng_d = din("ng", [128, 8])
    mg_d = din("mg", [128, 8])
    gout_d = din("gout", [128, 12])
    lbl_d = din("lbl", [128, 16])
    gfin_d = din("gfin", [128, D])
    posb_d = din("posb", [128, 2])

    yp_d = dout("yp", [NMAIN * 128, D])
    ys_d = dout("ys", [NSAMP * 64, D])
    shgp_d = dout("shgp", [8, 128, 128])
    srtp_d = dout("srtp", [4, 64, 128])
    mkp_d = dout("mkp", [NMEM, 512])
    mvp_d = dout("mvp", [NMEM, 512])
    shgs_d = dout("shgs", [NSAMP, 8, 128, 128])
    srts_d = dout("srts", [NSAMP, 4, 64, 128])

    with ExitStack() as es:
        P = Prog(nc, es)
        op = P.op

        def sb(name, shape, dt):
            return es.enter_context(nc.sbuf_tensor(name, list(shape), dt))

        win_sb = sb("win_sb", [128, 8, DIN], BF16)
        wout_sb = sb("wout_sb", [128, 16 * D], BF16)
        wout3 = wout_sb[:].rearrange("p (k n) -> p k n", n=D)
        wmem4 = wout_sb[:, 0:8192].rearrange("p (w k n) -> p w k n", w=2, k=8)
        Bx = wout_sb[:, 8192:9216].bitcast(F32)
        b_Bx = Buf()
        hk2 = wout_sb[:, 9216:10240].rearrange("p (k t) -> p k t", t=128)
        b_hk2 = Buf()
        b_win = [[Buf() for _ in range(13)] for _ in range(8)]
        b_wout = [Buf() for _ in range(16)]
        b_wmem = [Buf(), Buf()]

        ng_sb = sb("ng_sb", [128, 8], F32)
        mg_sb = sb("mg_sb", [128, 8], F32)
        gout_sb = sb("gout_sb", [128, 12], F32)
        lbl_sb = sb("lbl_sb", [128, 16], F32)
        posb_sb = sb("posb_sb", [128, 2], F32)
        gfin_sb = sb("gfin_sb", [128, D], F32)
        b_small = Buf()
        b_gfin = Buf()
        lbc = sb("lbc", [128, 32], F32)
        b_lbc = Buf()

        ident = sb("ident", [128, 128], BF16)
        onesf = sb("onesf", [128, 128], F32)
        Mc = sb("Mc", [128, 128], F32)
        Mrt = sb("Mrt", [128, 4, 128], F32)
        qdec = sb("qdec", [128, 2, 128], F32)
        kdec = sb("kdec", [128, 8], F32)
        b_const = Buf()

        rtab = sb("rtab", [128, 8, 32], F32)
        b_rtab = [Buf() for _ in range(4)]
        rtmp = sb("rtmp", [128, 6, 32], F32)
        b_rtmp = Buf()
        rti = sb("rti", [128, 32], I32)

        S_sb = sb("S_sb", [128, 8, 128], F32)
        Sbf = sb("Sbf", [128, 8, 128], BF16)
        R_sb = sb("R_sb", [128, 2, 128], F32)
        Rbf = sb("Rbf", [128, 2, 128], BF16)
        b_S = [Buf() for _ in range(8)]
        b_Sbf = Buf()
        b_R = Buf()
        b_Rbf = Buf()

        mkT = sb("mkT", [128, 2, 4, 128], BF16)
        mvA = sb("mvA", [128, 2, 4, 130], BF16)
        b_mkT = Buf()
        b_mvA = Buf()

        stage = [sb("stage%d" % i, [128, 512], F32) for i in range(2)]
        b_stage = [Buf() for _ in range(2)]
        NSTG = 2

        xt = [sb("xt%d" % i, [128, D], F32) for i in range(2)]
        b_xt = [Buf(), Buf()]
        scrA = sb("scrA", [128, DMIX], BF16)
        b_scrA = Buf()
        hk = sb("hk", [128, 8, 128], BF16)
        b_hk = Buf()
        BtAll = sb("BtAll", [128, 1536], F32)
        Bt = [BtAll[:, i * 512:(i + 1) * 512] for i in range(3)]
        b_Bt = [Buf(), Buf(), Buf()]
        b_otl = [b_Bt[1], b_Bt[2]]
        ot = BtAll[:, 512:1536]
        qk = sb("qk", [128, 16, 128], BF16)
        b_qk = Buf()
        xqT = sb("xqT", [128, 4, 128], BF16)
        b_xqT = Buf()
        v_bf = sb("v_bf", [128, D], BF16)
        b_v = Buf()
        sg_hg = sb("sg_hg", [128, D], BF16)
        sg_rt = sb("sg_rt", [128, 512], BF16)
        sg_xa = sb("sg_xa", [128, 512], BF16)
        b_sg = [Buf(), Buf(), Buf()]
        rv_bf = sb("rv_bf", [128, 512], BF16)
        b_rv = Buf()
        rop = sb("rop", [128, 2, 256], F32)
        b_rop = Buf()
        rqk = sb("rqk", [128, 512], BF16)
        b_rqk = Buf()
        rkd = sb("rkd", [128, 256], BF16)
        b_rkd = Buf()
        rqkT = sb("rqkT", [128, 4, 128], BF16)
        rqTm = sb("rqTm", [128, 4, 128], BF16)
        rqdTm = sb("rqdTm", [128, 4, 128], BF16)
        b_rqkT = Buf()
        AT = sb("AT", [128, 8, 128], BF16)
        b_AT = [Buf(), Buf()]
        PT = sb("PT", [128, 4, 128], BF16)
        b_PT = Buf()
        xP = sb("xP", [128, 8, 128], BF16)
        b_xP = [Buf(), Buf()]
        tn = sb("tn", [128, 512], F32)
        b_tn = Buf()
        junk = tn[:].bitcast(BF16)
        b_junk = b_tn
        b_jk = [b_tn, b_tn]
        tnx = tn
        b_tnx = b_tn
        st = sb("st", [128, 48], F32)
        b_st = [Buf() for _ in range(8)]
        Et = sb("Et", [128, 16], F32)
        b_E = Buf()
        b_E2 = Buf()
        b_qk2 = Buf()

        banks = [es.enter_context(nc.psum_tensor("bank%d" % i, [128, 512], F32)) for i in range(8)]
        b_bank = [Buf() for _ in range(8)]
        bank_i = [0]

        cx0 = dict(k=qk[:, 8:16, :], bk=b_qk, v=v_bf, bv=b_v, rv=rv_bf, brv=b_rv, rkd=rkd, brkd=b_rkd, E=0, bE=b_E,
                   kt=hk, bkt=[b_hk])
        cx1 = dict(k=qk[:, 0:8, :], bk=b_qk2, v=sg_hg, bv=b_sg[0], rv=sg_rt, brv=b_sg[1], rkd=sg_xa, brkd=b_sg[2], E=8, bE=b_E2,
                   kt=AT, bkt=b_AT)
        cx0p = dict(cx0, kt=AT, bkt=b_AT)

        class BankPool:
            def __init__(self, idxs):
                self.idxs = list(idxs)
                self.i = 0

            def next(self):
                k = self.idxs[self.i % len(self.idxs)]
                self.i += 1
                return banks[k], b_bank[k]

        pool_all = BankPool(range(8))

        def nb(pool=None):
            return (pool or pool_all).next()

        op("sp", lambda e: e.dma_start(out=ng_sb[:], in_=ng_d), writes=[b_small])
        op("sp", lambda e: e.dma_start(out=mg_sb[:], in_=mg_d), writes=[b_small])
        op("sp", lambda e: e.dma_start(out=gout_sb[:], in_=gout_d), writes=[b_small])
        op("sp", lambda e: e.dma_start(out=lbl_sb[:], in_=lbl_d), writes=[b_small])
        op("sp", lambda e: e.dma_start(out=posb_sb[:], in_=posb_d), writes=[b_small])
        op("sp", lambda e: e.dma_start(out=gfin_sb[:], in_=gfin_d), writes=[b_gfin])

        op("pool", lambda e: e.memset(onesf[:], 1.0), writes=[b_const])
        op("pool", lambda e: e.affine_select(out=ident[:], in_=onesf[:], pattern=[[1, 128]], compare_op=ALU.is_equal,
                                             fill=0.0, base=0, channel_multiplier=-1), reads=[b_const], writes=[b_const])
        op("pool", lambda e: e.affine_select(out=Mc[:], in_=onesf[:], pattern=[[1, 128]], compare_op=ALU.is_ge,
                                             fill=0.0, base=0, channel_multiplier=-1), reads=[b_const], writes=[b_const])
        dij = tn
        dii = hk[:].rearrange("p a b -> p (a b)").bitcast(I32)[:, 0:128]
        b_dii = b_hk
        op("pool", lambda e: e.iota(dii[:], pattern=[[1, 128]], base=0, channel_multiplier=-1), writes=[b_dii])
        op("pool", lambda e: e.tensor_copy(out=dij[:, 0:128], in_=dii[:]), reads=[b_dii], writes=[b_tn])
        op("dve", lambda e: e.scalar_tensor_tensor(out=dij[:, 128:256], in0=dij[:, 0:128], scalar=-1.0, in1=dij[:, 0:128],
                                                   op0=ALU.mult, op1=ALU.max), reads=[b_tn], writes=[b_tn])
        for h in range(4):
            op("act", lambda e, h=h: e.activation(out=Mrt[:, h, :], in_=dij[:, 128:256], func=AF.Exp, scale=LG[h]),
               reads=[b_tn], writes=[b_const])
        op("pool", lambda e: e.memset(Mrt[64:128, :, 0:64], 0.0), reads=[b_const], writes=[b_const])
        op("pool", lambda e: e.iota(dii[:], pattern=[[1, 128]], base=1, channel_multiplier=0), reads=[b_dii], writes=[b_dii])
        op("pool", lambda e: e.tensor_copy(out=dij[:, 256:384], in_=dii[:]), reads=[b_dii], writes=[b_tn])
        for h in range(4):
            hb = (h % 2) * 64
            op("act", lambda e, h=h, hb=hb: e.activation(out=qdec[hb:hb + 64, h // 2, :], in_=dij[hb:hb + 64, 256:384],
                                                         func=AF.Exp, scale=LG[h]), reads=[b_tn], writes=[b_const])
        for ti, T in enumerate((128, 64)):
            op("pool", lambda e, T=T: e.iota(dii[:, 0:1], pattern=[[0, 1]], base=T - 1, channel_multiplier=-1),
               reads=[b_dii], writes=[b_dii])
            op("pool", lambda e: e.tensor_copy(out=dij[:, 384:385], in_=dii[:, 0:1]), reads=[b_dii], writes=[b_tn])
            for h in range(4):
                op("act", lambda e, h=h, ti=ti: e.activation(out=kdec[:, ti * 4 + h:ti * 4 + h + 1], in_=dij[:, 384:385],
                                                             func=AF.Exp, scale=LG[h]), reads=[b_tn], writes=[b_const])
        op("dve", lambda e: e.tensor_tensor(out=lbc[:, 24:32], in0=lbl_sb[:, 8:16], in1=lbl_sb[:, 0:8], op=ALU.subtract),
           reads=[b_small], writes=[b_lbc])
        op("act", lambda e: e.activation(out=lbc[:, 24:32], in_=lbc[:, 24:32], func=AF.Exp), reads=[b_lbc], writes=[b_lbc])
        op("dve", lambda e: e.tensor_scalar_add(out=lbc[:, 24:32], in0=lbc[:, 24:32], scalar1=1.0), reads=[b_lbc], writes=[b_lbc])
        op("dve", lambda e: e.reciprocal(out=lbc[:, 24:32], in_=lbc[:, 24:32]), reads=[b_lbc], writes=[b_lbc])
        op("dve", lambda e: e.tensor_scalar(out=lbc[:, 0:8], in0=lbc[:, 24:32], scalar1=0.5, scalar2=0.5, op0=ALU.mult, op1=ALU.add),
           reads=[b_lbc], writes=[b_lbc])
        op("dve", lambda e: e.tensor_scalar(out=lbc[:, 8:16], in0=lbc[:, 24:32], scalar1=-0.5, scalar2=0.5, op0=ALU.mult, op1=ALU.add),
           reads=[b_lbc], writes=[b_lbc])
        op("dve", lambda e: e.tensor_scalar(out=lbc[:, 16:24], in0=lbc[:, 24:32], scalar1=0.5, scalar2=-0.5, op0=ALU.mult, op1=ALU.add),
           reads=[b_lbc], writes=[b_lbc])

        stg_i = [0]

        cast_engs = ["pool"]

        def wjob(dram_ap, dst_ap, rows_scale_ap, colscale, reads, writes):
            i = stg_i[0] % len(stage)
            eng = cast_engs[stg_i[0] % len(cast_engs)]
            stg_i[0] += 1
            n = dst_ap.shape[-1]
            src = stage[i][:, 0:n]
            bst = b_stage[i]
            op("sp", lambda e: e.dma_start(out=src, in_=dram_ap), writes=[bst])
            if eng == "act" and colscale != 1.0:
                eng = "dve"
            if rows_scale_ap is None:
                if eng == "act":
                    op("act", lambda e: e.activation(out=dst_ap, in_=src, func=AF.Copy), reads=[bst] + reads, writes=writes)
                else:
                    op(eng, lambda e: e.tensor_copy(out=dst_ap, in_=src), reads=[bst] + reads, writes=writes)
            elif eng == "act":
                op("act", lambda e: e.activation(out=dst_ap, in_=src, func=AF.Copy, scale=rows_scale_ap),
                   reads=[bst, b_small] + reads, writes=writes)
            else:
                op(eng, lambda e: e.tensor_scalar(out=dst_ap, in0=src, scalar1=rows_scale_ap, scalar2=colscale,
                                                  op0=ALU.mult, op1=ALU.mult), reads=[bst, b_small] + reads, writes=writes)

        def win_job(kc, c0, n, colscale=1.0):
            wjob(win_d[kc * 128:(kc + 1) * 128, c0:c0 + n], win_sb[:, kc, c0:c0 + n], ng_sb[:, kc:kc + 1], colscale,
                 [], [b_win[kc][c0 // 512]])

        pre_cols = [(C_F, 512), (C_F + 512, 512), (C_I, 512), (C_I + 512, 512), (C_RK, 256), (C_RV, 512)]
        rest_cols = [(C_Q, 512), (C_Q + 512, 512), (C_XQ, 512), (C_G, 512), (C_G + 512, 512), (C_RQ, 256),
                     (C_RG, 512), (C_XG, 512)]
        cast_engs[:] = ["dve", "act", "pool", "dve", "act"]
        stage_all = list(stage)
        bstage_all = list(b_stage)
        stage.extend([xt[0][:, 0:512], xt[0][:, 512:1024], xt[1][:, 0:512], xt[1][:, 512:1024]])
        b_sx = [Buf() for _ in range(4)]
        b_stage.extend(b_sx)
        NSTG = 6
        for c0, n in pre_cols:
            for kc in range(8):
                win_job(kc, c0, n, 0.125 if c0 == C_RK else 1.0)
        cast_engs[:] = ["pool"]
        del stage[2:]
        del b_stage[2:]
        NSTG = 2

        om = rtmp[:, 0, :]
        op("pool", lambda e: e.iota(rti[:], pattern=[[1, 32]], base=0, channel_multiplier=0), writes=[b_rtmp])
        op("pool", lambda e: e.tensor_copy(out=rtmp[:, 0, :], in_=rti[:]), reads=[b_rtmp], writes=[b_rtmp])
        op("act", lambda e: e.activation(out=rtmp[:, 0, :], in_=rtmp[:, 0, :], func=AF.Exp, scale=-math.log(10000.0) / 32.0),
           reads=[b_rtmp], writes=[b_rtmp])

        a6 = tn[:, 0:192]
        q6 = tn[:, 192:384]
        m6 = BtAll[:, 0:192]
        i6 = hk[:].rearrange("p a b -> p (a b)").bitcast(I32)[:, 0:192]
        b_r6 = [b_tn, b_Bt[0], b_hk]
        specs = [(posb_sb[:, 0:1], 0.0), (posb_sb[:, 0:1], math.pi / 2), (posb_sb[:, 1:2], 0.0), (posb_sb[:, 1:2], math.pi / 2),
                 (128.0, 0.0), (128.0, math.pi / 2)]
        for k, (pm, sh) in enumerate(specs):
            op("dve", lambda e, k=k, pm=pm, sh=sh: e.tensor_scalar(out=a6[:, k * 32:(k + 1) * 32], in0=om, scalar1=pm, scalar2=sh,
                                                                    op0=ALU.mult, op1=ALU.add),
               reads=[b_rtmp, b_small], writes=b_r6)
        op("dve", lambda e: e.tensor_scalar_mul(out=q6, in0=a6, scalar1=1.0 / TWO_PI), reads=b_r6, writes=b_r6)
        op("dve", lambda e: e.tensor_copy(out=i6, in_=q6), reads=b_r6, writes=b_r6)
        op("dve", lambda e: e.tensor_copy(out=q6, in_=i6), reads=b_r6, writes=b_r6)
        op("dve", lambda e: e.scalar_tensor_tensor(out=a6, in0=q6, scalar=-CW1, in1=a6, op0=ALU.mult, op1=ALU.add), reads=b_r6, writes=b_r6)
        op("dve", lambda e: e.scalar_tensor_tensor(out=a6, in0=q6, scalar=-CW2, in1=a6, op0=ALU.mult, op1=ALU.add), reads=b_r6, writes=b_r6)
        for _ in range(2):
            op("dve", lambda e: e.tensor_single_scalar(out=m6, in_=a6, scalar=math.pi, op=ALU.is_gt), reads=b_r6, writes=b_r6)
            op("dve", lambda e: e.scalar_tensor_tensor(out=a6, in0=m6, scalar=-TWO_PI, in1=a6, op0=ALU.mult, op1=ALU.add), reads=b_r6, writes=b_r6)
            op("dve", lambda e: e.tensor_single_scalar(out=m6, in_=a6, scalar=-math.pi, op=ALU.is_lt), reads=b_r6, writes=b_r6)
            op("dve", lambda e: e.scalar_tensor_tensor(out=a6, in0=m6, scalar=TWO_PI, in1=a6, op0=ALU.mult, op1=ALU.add), reads=b_r6, writes=b_r6)
        op("act", lambda e: e.activation(out=rtab[:, 0:2, :], in_=a6[:, 0:64].rearrange("p (k i) -> p k i", i=32), func=AF.Sin),
           reads=b_r6, writes=[b_rtab[0]])
        op("act", lambda e: e.activation(out=rtab[:, 4:8, :], in_=a6[:, 64:192].rearrange("p (k i) -> p k i", i=32), func=AF.Sin),
           reads=b_r6, writes=[b_rtab[2], b_rtab[3]])

        bg_jobs = []
        for w, wd in enumerate((wmk_d, wmv_d)):
            for kc in range(8):
                bg_jobs.append(lambda w=w, wd=wd, kc=kc: wjob(wd[kc * 128:(kc + 1) * 128, :], wmem4[:, w, kc, :],
                                                            mg_sb[:, kc:kc + 1], 1.0, [], [b_wmem[w]]))
        for kc in range(8):
            for c0, n in rest_cols:
                bg_jobs.append(lambda kc=kc, c0=c0, n=n: win_job(kc, c0, n))
        wout_jobs = []
        for kc in range(16):
            for nbk in range(2):
                if kc < 12:
                    wout_jobs.append(lambda kc=kc, nbk=nbk: wjob(wout_d[kc * 128:(kc + 1) * 128, nbk * 512:(nbk + 1) * 512],
                                                                 wout3[:, kc, nbk * 512:(nbk + 1) * 512],
                                                                 gout_sb[:, kc:kc + 1], 1.0, [], [b_wout[kc], b_Bx, b_hk2] + b_wmem))
                else:
                    wout_jobs.append(lambda kc=kc, nbk=nbk: wjob(wout_d[kc * 128:(kc + 1) * 128, nbk * 512:(nbk + 1) * 512],
                                                                 wout3[:, kc, nbk * 512:(nbk + 1) * 512],
                                                                 None, 1.0, [], [b_wout[kc], b_Bx, b_hk2] + b_wmem))

        def wr(kc, c0):
            return b_win[kc][c0 // 512]

        def rms_rstd(src_ap, src_buf, T, stcol, scale):
            sc = st[0:T, stcol:stcol + 1]
            op("act", lambda e: e.activation(out=junk[0:T, 0:src_ap.shape[-1]], in_=src_ap, func=AF.Square, accum_out=sc),
               reads=[src_buf], writes=[b_junk, b_st[0]])
            op("act", lambda e: e.activation(out=sc, in_=sc, func=AF.Ln, bias=EPS, scale=scale), reads=[b_st[0]], writes=[b_st[0]])
            op("act", lambda e: e.activation(out=sc, in_=sc, func=AF.Exp, scale=-0.5), reads=[b_st[0]], writes=[b_st[0]])

        def norm_transpose(src_ap, src_buf, T):
            rms_rstd(src_ap, src_buf, T, 0, 1.0 / D)
            op("dve", lambda e: e.tensor_scalar_mul(out=scrA[0:T, 0:D], in0=src_ap, scalar1=st[0:T, 0:1]),
               reads=[src_buf, b_st[0]], writes=[b_scrA])
            bk, bb = nb()
            bkb = bk[:].bitcast(BF16)
            for kc in range(8):
                op("pe", lambda e, kc=kc: e.transpose(out=bkb[:, kc * T:(kc + 1) * T], in_=scrA[0:T, kc * 128:(kc + 1) * 128],
                                                      identity=ident[0:T, 0:T]), reads=[b_scrA, b_const], writes=[bb])
            op("act", lambda e: e.activation(out=hk[:, :, 0:T], in_=bkb[:, 0:8 * T].rearrange("p (k t) -> p k t", t=T), func=AF.Copy),
               reads=[bb], writes=[b_hk])

        def fm_group(c0, T, pool=None, hT=None, bhT=None):
            bk, bb = nb(pool)
            hT = hk if hT is None else hT
            bhT = b_hk if bhT is None else bhT
            for hh in range(4):
                for kc in range(8):
                    op("pe", lambda e, hh=hh, kc=kc: e.matmul(bk[:, hh * T:(hh + 1) * T],
                                                              win_sb[:, kc, c0 + hh * 128:c0 + (hh + 1) * 128], hT[:, kc, 0:T],
                                                              start=(kc == 0), stop=(kc == 7)),
                       reads=[bhT, wr(kc, c0)], writes=[bb])
            return bk, bb

        def tm_group(c0, n, T, col_off=0, pool=None, hT=None, bhT=None):
            bk, bb = nb(pool)
            hT = hk if hT is None else hT
            bhT = b_hk if bhT is None else bhT
            for kc in range(8):
                op("pe", lambda e, kc=kc: e.matmul(bk[0:T, col_off:col_off + n], hT[:, kc, 0:T], win_sb[:, kc, c0:c0 + n],
                                                   start=(kc == 0), stop=(kc == 7)),
                   reads=[bhT, wr(kc, c0)], writes=[bb])
            return bk, bb

        def f_chain(hf, T, need_ep, cx):
            bk, bb = fm_group(C_F + hf * 512, T)
            n = 4 * T
            th, kk, gc = Bt[0], Bt[1], Bt[2]
            op("act", lambda e: e.activation(out=th[:, 0:n], in_=bk[:, 0:n], func=AF.Tanh, scale=0.5), reads=[bb], writes=[b_Bt[0]])
            for hh in range(4):
                h = hf * 4 + hh
                op("dve", lambda e, hh=hh, h=h: e.tensor_scalar(out=kk[:, hh * T:(hh + 1) * T], in0=th[:, hh * T:(hh + 1) * T],
                                                                scalar1=lbc[:, 16 + h:17 + h], scalar2=lbc[:, 8 + h:9 + h],
                                                                op0=ALU.mult, op1=ALU.add),
                   reads=[b_Bt[0], b_lbc], writes=[b_Bt[1]])
            for hh in range(4):
                h = hf * 4 + hh
                op("act", lambda e, hh=hh, h=h: e.activation(out=th[:, hh * T:(hh + 1) * T], in_=th[:, hh * T:(hh + 1) * T], func=AF.Ln,
                                                             bias=lbc[:, h:h + 1], scale=lbc[:, 8 + h:9 + h]),
                   reads=[b_Bt[0], b_lbc], writes=[b_Bt[0]])
            for hh in range(4):
                op("dve", lambda e, hh=hh: e.tensor_tensor_scan(out=gc[:, hh * T:(hh + 1) * T], data0=onesf[:, 0:T],
                                                                data1=th[:, hh * T:(hh + 1) * T], initial=0.0,
                                                                op0=ALU.mult, op1=ALU.add),
                   reads=[b_Bt[0], b_const], writes=[b_Bt[2]])
            gc3 = gc[:, 0:n].rearrange("p (h t) -> p h t", t=T)
            e0 = cx["E"] + hf * 4
            op("act", lambda e: e.activation(out=Et[:, e0:e0 + 4], in_=gc3[:, :, T - 1], func=AF.Exp),
               reads=[b_Bt[2]], writes=[cx["bE"]])
            if need_ep:
                op("act", lambda e: e.activation(out=th[:, 0:n], in_=gc[:, 0:n], func=AF.Exp), reads=[b_Bt[2]], writes=[b_Bt[0]])
            op("act", lambda e: e.activation(out=gc[:, 0:n], in_=gc[:, 0:n], func=AF.Exp, scale=-1.0), reads=[b_Bt[2]], writes=[b_Bt[2]])
            op("pool", lambda e: e.tensor_tensor(out=cx["k"][:, hf * 4:hf * 4 + 4, 0:T], in0=kk[:, 0:n].rearrange("p (h t) -> p h t", t=T),
                                                 in1=gc3, op=ALU.mult), reads=[b_Bt[1], b_Bt[2]], writes=[cx["bk"]])

        def q_chain(hf, T):
            bk, bb = fm_group(C_Q + hf * 512, T)
            n = 4 * T
            qs = Bt[1]
            op("act", lambda e: e.activation(out=qs[:, 0:n], in_=bk[:, 0:n], func=AF.Silu), reads=[bb], writes=[b_Bt[1]])
            op("pool", lambda e: e.tensor_tensor(out=qk[:, hf * 4:hf * 4 + 4, 0:T], in0=qs[:, 0:n].rearrange("p (h t) -> p h t", t=T),
                                                 in1=Bt[0][:, 0:n].rearrange("p (h t) -> p h t", t=T), op=ALU.mult),
               reads=[b_Bt[1], b_Bt[0]], writes=[b_qk, b_qk2])

        def rope(bk, bb, T, g0, g1, tabi, tabbuf):
            ng_ = g1 - g0
            X = bk[0:T, g0 * 64:g1 * 64].rearrange("p (g two i) -> p g two i", two=2, i=32)
            O = rqk[0:T, g0 * 64:g1 * 64].rearrange("p (g two i) -> p g two i", two=2, i=32)
            sn = rtab[0:T, tabi, :].unsqueeze(1).broadcast_to([T, ng_, 32])
            cs = rtab[0:T, tabi + 1, :].unsqueeze(1).broadcast_to([T, ng_, 32])
            t1 = rop[0:T, 0, 0:ng_ * 32].rearrange("p (g i) -> p g i", i=32)
            t2 = rop[0:T, 1, 0:ng_ * 32].rearrange("p (g i) -> p g i", i=32)
            rd = [bb, tabbuf]
            op("dve", lambda e: e.tensor_tensor(out=t1, in0=X[:, :, 0, :], in1=cs, op=ALU.mult), reads=rd, writes=[b_rop])
            op("dve", lambda e: e.tensor_tensor(out=t2, in0=X[:, :, 1, :], in1=sn, op=ALU.mult), reads=rd, writes=[b_rop])
            op("dve", lambda e: e.tensor_tensor(out=O[:, :, 0, :], in0=t1, in1=t2, op=ALU.subtract), reads=[b_rop], writes=[b_rqk])
            op("dve", lambda e: e.tensor_tensor(out=t1, in0=X[:, :, 0, :], in1=sn, op=ALU.mult), reads=rd + [b_rqk], writes=[b_rop])
            op("dve", lambda e: e.tensor_tensor(out=t2, in0=X[:, :, 1, :], in1=cs, op=ALU.mult), reads=rd, writes=[b_rop])
            op("dve", lambda e: e.tensor_tensor(out=O[:, :, 1, :], in0=t1, in1=t2, op=ALU.add), reads=[b_rop], writes=[b_rqk])

        def rope_advance(cur):
            s0, c0_ = rtab[:, 2 * cur, :], rtab[:, 2 * cur + 1, :]
            s1, c1_ = rtab[:, 2 * (1 - cur), :], rtab[:, 2 * (1 - cur) + 1, :]
            S, C = rtab[:, 6, :], rtab[:, 7, :]
            t = [rtmp[:, k, :] for k in range(1, 5)]
            rd = [b_rtab[cur], b_rtab[3]]
            op("pool", lambda e: e.tensor_tensor(out=t[0], in0=c0_, in1=C, op=ALU.mult), reads=rd, writes=[b_rtmp])
            op("pool", lambda e: e.tensor_tensor(out=t[1], in0=s0, in1=S, op=ALU.mult), reads=rd, writes=[b_rtmp])
            op("pool", lambda e: e.tensor_tensor(out=t[2], in0=s0, in1=C, op=ALU.mult), reads=rd, writes=[b_rtmp])
            op("pool", lambda e: e.tensor_tensor(out=t[3], in0=c0_, in1=S, op=ALU.mult), reads=rd, writes=[b_rtmp])
            op("pool", lambda e: e.tensor_tensor(out=c1_, in0=t[0], in1=t[1], op=ALU.subtract), reads=[b_rtmp], writes=[b_rtab[1 - cur]])
            op("pool", lambda e: e.tensor_tensor(out=s1, in0=t[2], in1=t[3], op=ALU.add), reads=[b_rtmp], writes=[b_rtab[1 - cur]])

        def state_hg_T(T, cx, pool=None):
            bk, bb = nb(pool)
            bkb = bk[:].bitcast(BF16)
            for h in range(8):
                op("pe", lambda e, h=h: e.transpose(out=bkb[0:T, h * 128:(h + 1) * 128], in_=cx["k"][:, h, 0:T], identity=ident[:, :]),
                   reads=[cx["bk"], b_const], writes=[bb])
            op("dve", lambda e: e.tensor_copy(out=cx["kt"][0:T, :, :], in_=bkb[0:T, :].rearrange("p (h d) -> p h d", d=128)),
               reads=[bb], writes=cx["bkt"])

        def state_hg_U_g(T, want_bf, cx, gen=True, pool=None):
            for hf in range(2):
                bk2, bb2 = nb(pool)
                for hh in range(4):
                    h = hf * 4 + hh
                    op("pe", lambda e, hh=hh, h=h, bk2=bk2: e.matmul(bk2[:, hh * 128:(hh + 1) * 128], cx["kt"][0:T, h, :],
                                                                     cx["v"][0:T, h * 128:(h + 1) * 128], start=True, stop=True),
                       reads=cx["bkt"] + [cx["bv"]], writes=[bb2])
                if gen:
                    yield
                S2 = S_sb[:, hf * 4:hf * 4 + 4, :]
                op("dve", lambda e, S2=S2, bk2=bk2: e.tensor_tensor(out=S2, in0=S2, in1=bk2[:, :].rearrange("p (h v) -> p h v", v=128),
                                                                     op=ALU.add),
                   reads=[bb2] + b_S[hf * 4:hf * 4 + 4], writes=b_S[hf * 4:hf * 4 + 4])
                ec = cx["E"] + hf * 4
                op("dve", lambda e, S2=S2, ec=ec: e.tensor_tensor(out=S2, in0=S2, in1=Et[:, ec:ec + 4].unsqueeze(2).broadcast_to([128, 4, 128]),
                                                                   op=ALU.mult),
                   reads=[cx["bE"]] + b_S[hf * 4:hf * 4 + 4], writes=b_S[hf * 4:hf * 4 + 4])
                if gen:
                    yield
            if want_bf:
                op("pool", lambda e: e.tensor_copy(out=Sbf[:], in_=S_sb[:]), reads=b_S, writes=[b_Sbf])

        def state_hg_U(T, want_bf, cx):
            for _ in state_hg_U_g(T, want_bf, cx):
                pass

        def state_update_hg(T, want_bf, cx):
            state_hg_T(T, cx)
            state_hg_U(T, want_bf, cx)

        def make_rkd(T, cx):
            ti = 0 if T == 128 else 1
            op("dve", lambda e: e.tensor_tensor(out=cx["rkd"][0:T, 0:256].rearrange("p (h d) -> p h d", d=64),
                                                in0=rqk[0:T, 256:512].rearrange("p (h d) -> p h d", d=64),
                                                in1=kdec[0:T, ti * 4:ti * 4 + 4].unsqueeze(2).broadcast_to([T, 4, 64]), op=ALU.mult),
               reads=[b_rqk, b_const], writes=[cx["brkd"]])

        def state_update_rt(T, want_bf, cx, pool=None):
            bk, bb = nb(pool)
            for pr in range(2):
                op("pe", lambda e, pr=pr: e.matmul(bk[:, pr * 256:(pr + 1) * 256], cx["rkd"][0:T, pr * 128:(pr + 1) * 128],
                                                   cx["rv"][0:T, pr * 256:(pr + 1) * 256], start=True, stop=True),
                   reads=[cx["brkd"], cx["brv"]], writes=[bb])
            for h in range(4):
                hb = (h % 2) * 64
                pr = h // 2
                g = math.exp(LG[h] * T)
                op("dve", lambda e, hb=hb, pr=pr, g=g, h=h: e.scalar_tensor_tensor(
                    out=R_sb[hb:hb + 64, pr, :], in0=R_sb[hb:hb + 64, pr, :], scalar=g,
                    in1=bk[hb:hb + 64, pr * 256 + (h % 2) * 128:pr * 256 + (h % 2) * 128 + 128], op0=ALU.mult, op1=ALU.add),
                   reads=[bb, b_R], writes=[b_R])
            if want_bf:
                op("pool", lambda e: e.tensor_copy(out=Rbf[:], in_=R_sb[:]), reads=[b_R], writes=[b_Rbf])

        x_extra = {}

        def load_x(x_dram_rows, T, slot):
            op("sp", lambda e: e.dma_start(out=xt[slot][0:T, :], in_=x_dram_rows), writes=[b_xt[slot]] + x_extra.pop(slot, []))

        b_mix = [Buf(), Buf(), Buf()]

        def chain(*gens):
            for g in gens:
                for _ in g:
                    yield

        def fq_half_gen(T, cx, hf, B3, bB3, pool):
            n = 4 * T
            Bu, Bg, Bw = B3
            bu, bg, bw = bB3
            bkf, bbf = fm_group(C_F + hf * 512, T, pool)
            yield
            bkq, bbq = fm_group(C_Q + hf * 512, T, pool)
            yield
            op("act", lambda e: e.activation(out=Bu[:, 0:n], in_=bkf[:, 0:n], func=AF.Exp, scale=-1.0), reads=[bbf], writes=bu)
            op("act", lambda e: e.activation(out=Bw[:, 0:n], in_=bkq[:, 0:n], func=AF.Exp, scale=-1.0), reads=[bbq], writes=bw)
            yield
            for hh in range(4):
                h = hf * 4 + hh
                op("act", lambda e, hh=hh, h=h: e.activation(out=Bg[:, hh * T:(hh + 1) * T], in_=Bu[:, hh * T:(hh + 1) * T], func=AF.Ln,
                                                             bias=1.0, scale=lbc[:, 24 + h:25 + h]),
                   reads=bu + [b_lbc], writes=bg)
            yield
            op("act", lambda e: e.activation(out=Bu[:, 0:n], in_=Bu[:, 0:n], func=AF.Ln, bias=1.0, scale=1.0), reads=bu, writes=bu)
            op("act", lambda e: e.activation(out=Bw[:, 0:n], in_=Bw[:, 0:n], func=AF.Ln, bias=1.0, scale=1.0), reads=bw, writes=bw)
            yield
            op("dve", lambda e: e.tensor_tensor(out=Bg[:, 0:n], in0=Bg[:, 0:n], in1=Bu[:, 0:n], op=ALU.subtract),
               reads=bg + bu, writes=bg)
            yield
            for hh in range(4):
                op("dve", lambda e, hh=hh: e.tensor_tensor_scan(out=Bu[:, hh * T:(hh + 1) * T], data0=onesf[:, 0:T],
                                                                data1=Bg[:, hh * T:(hh + 1) * T], initial=0.0,
                                                                op0=ALU.mult, op1=ALU.add),
                   reads=bg + [b_const], writes=bu)
                yield
            gc3 = Bu[:, 0:n].rearrange("p (h t) -> p h t", t=T)
            e0 = cx["E"] + hf * 4
            op("act", lambda e: e.activation(out=Et[:, e0:e0 + 4], in_=gc3[:, :, T - 1], func=AF.Exp),
               reads=bu, writes=[cx["bE"]])
            op("act", lambda e: e.activation(out=Bg[:, 0:n], in_=Bu[:, 0:n], func=AF.Exp, scale=-1.0), reads=bu, writes=bg)
            op("dve", lambda e: e.tensor_tensor(out=Bw[:, 0:n], in0=Bu[:, 0:n], in1=Bw[:, 0:n], op=ALU.subtract),
               reads=bu + bw, writes=bw)
            yield
            op("act", lambda e: e.activation(out=Bw[:, 0:n], in_=Bw[:, 0:n], func=AF.Exp), reads=bw, writes=bw)
            em3 = Bg[:, 0:n].rearrange("p (h t) -> p h t", t=T)
            kd = cx["k"][:, hf * 4:hf * 4 + 4, :]
            op("dve", lambda e: e.tensor_tensor(out=kd[:, :, 1:T], in0=em3[:, :, 1:T], in1=em3[:, :, 0:T - 1], op=ALU.subtract),
               reads=bg, writes=[cx["bk"]])
            op("dve", lambda e: e.tensor_scalar_add(out=kd[:, :, 0], in0=em3[:, :, 0], scalar1=-1.0),
               reads=bg, writes=[cx["bk"]])
            yield
            op("dve", lambda e: e.tensor_tensor(out=qk[:, hf * 4:hf * 4 + 4, 0:T],
                                                in0=bkq[:, 0:n].rearrange("p (h t) -> p h t", t=T),
                                                in1=Bw[:, 0:n].rearrange("p (h t) -> p h t", t=T), op=ALU.mult),
               reads=[bbq] + bw, writes=[b_qk, b_qk2])
            yield

        def fq_gen(T, cx):
            pool = BankPool([0, 1])
            B3 = (Bt[0], Bt[1], Bt[2])
            bB3 = ([b_Bt[0]], [b_Bt[1]], [b_Bt[2]])
            return chain(fq_half_gen(T, cx, 0, B3, bB3, pool), fq_half_gen(T, cx, 1, B3, bB3, pool))

        def fq_par_gens(T, cx):
            B3a = (Bt[0], Bt[1], Bt[2])
            bB3a = ([b_Bt[0]], [b_Bt[1]], [b_Bt[2]])
            B3b = (AT[:].rearrange("p a b -> p (a b)").bitcast(F32), xP[:].rearrange("p a b -> p (a b)").bitcast(F32), tn[:, :])
            bB3b = (list(b_AT), list(b_xP), [b_tn])
            return [fq_half_gen(T, cx, 0, B3a, bB3a, BankPool([0, 1])), fq_half_gen(T, cx, 1, B3b, bB3b, BankPool([2, 3]))]

        def tok_gen(T, cx, tabi, tabbuf, banks=(2, 3, 4, 5, 6)):
            pool = BankPool(banks)
            for half in range(2):
                bk, bb = tm_group(C_I + half * 512, 512, T, 0, pool)
                op("act", lambda e, bk=bk, half=half: e.activation(out=v_bf[0:T, half * 512:(half + 1) * 512], in_=bk[0:T, :], func=AF.Copy),
                   reads=[bb], writes=[b_v])
                yield
            bk, bb = tm_group(C_RQ, 512, T, 0, pool)
            yield
            rope(bk, bb, T, 0, 8, tabi, tabbuf)
            yield
            make_rkd(T, cx)
            bk, bb = tm_group(C_RV, 512, T, 0, pool)
            yield
            op("dve", lambda e, bk=bk: e.tensor_copy(out=rv_bf[0:T, :], in_=bk[0:T, :]), reads=[bb], writes=[b_rv])
            yield
            bk, bb = fm_group(C_XQ, T, pool)
            op("act", lambda e, bk=bk: e.activation(out=xqT[:, :, 0:T], in_=bk[:, 0:4 * T].rearrange("p (h t) -> p h t", t=T),
                                                    func=AF.Copy, scale=128.0 ** -0.5), reads=[bb], writes=[b_xqT])
            yield
            gb = []
            for half in range(2):
                gb.append(tm_group(C_G + half * 512, 512, T, 0, pool))
                yield
            gb.append(tm_group(C_RG, 512, T, 0, pool))
            yield
            gb.append(tm_group(C_XG, 512, T, 0, pool))
            yield
            for half in range(2):
                bk, bb = gb[half]
                op("act", lambda e, bk=bk, half=half: e.activation(out=sg_hg[0:T, half * 512:(half + 1) * 512], in_=bk[0:T, :], func=AF.Silu),
                   reads=[bb], writes=[b_sg[0]])
            bk, bb = gb[2]
            op("act", lambda e, bk=bk: e.activation(out=sg_rt[0:T, :], in_=bk[0:T, :], func=AF.Silu), reads=[bb], writes=[b_sg[1]])
            bk, bb = gb[3]
            op("act", lambda e, bk=bk: e.activation(out=sg_xa[0:T, :], in_=bk[0:T, :], func=AF.Silu), reads=[bb], writes=[b_sg[2]])
            yield

        def hgrn_gen(T, want_bf, cx):
            pool = BankPool([0, 1, 2, 3])
            sb_ = []
            for hf in range(2):
                bk, bb = nb(pool)
                for hh in range(4):
                    h = hf * 4 + hh
                    op("pe", lambda e, bk=bk, hh=hh, h=h: e.matmul(bk[0:T, hh * T:(hh + 1) * T], qk[:, 8 + h, 0:T], qk[:, h, 0:T],
                                                                   start=True, stop=True), reads=[b_qk], writes=[bb])
                sb_.append((bk, bb))
                yield
            for hf in range(2):
                bk, bb = sb_[hf]
                op("dve", lambda e, bk=bk, hf=hf: e.tensor_tensor(out=AT[0:T, hf * 4:hf * 4 + 4, 0:T],
                                                                 in0=bk[0:T, 0:4 * T].rearrange("p (h t) -> p h t", t=T),
                                                                 in1=Mc[0:T, 0:T].unsqueeze(1).broadcast_to([T, 4, T]), op=ALU.mult),
                   reads=[bb, b_const], writes=[b_AT[hf]])
                yield
            obanks = []
            for hf in range(2):
                bk, bb = nb(pool)
                for hh in range(4):
                    h = hf * 4 + hh
                    op("pe", lambda e, bk=bk, hh=hh, h=h: e.matmul(bk[0:T, hh * 128:(hh + 1) * 128], AT[0:T, h, 0:T],
                                                                   v_bf[0:T, h * 128:(h + 1) * 128], start=True, stop=False),
                       reads=[b_AT[hf], b_v], writes=[bb])
                    op("pe", lambda e, bk=bk, hh=hh, h=h: e.matmul(bk[0:T, hh * 128:(hh + 1) * 128], qk[:, h, 0:T], Sbf[:, h, :],
                                                                   start=False, stop=True), reads=[b_qk, b_Sbf], writes=[bb])
                obanks.append((bk, bb))
                yield
            state_hg_T(T, cx, pool)
            yield
            for hf in range(2):
                bk, bb = obanks[hf]
                for hh in range(4):
                    h = hf * 4 + hh
                    op("act", lambda e, bk=bk, hh=hh, h=h: e.activation(out=junk[0:T, h * 128:(h + 1) * 128], in_=bk[0:T, hh * 128:(hh + 1) * 128],
                                                                        func=AF.Square, accum_out=st[0:T, 8 + h:9 + h]),
                       reads=[bb], writes=[b_jk[0], b_st[1]])
                yield
            op("act", lambda e: e.activation(out=st[0:T, 8:16], in_=st[0:T, 8:16], func=AF.Ln, bias=EPS, scale=1.0 / 128), reads=[b_st[1]], writes=[b_st[1]])
            op("act", lambda e: e.activation(out=st[0:T, 8:16], in_=st[0:T, 8:16], func=AF.Exp, scale=-0.5), reads=[b_st[1]], writes=[b_st[1]])
            yield
            for hf in range(2):
                bk, bb = obanks[hf]
                for hh in range(4):
                    h = hf * 4 + hh
                    op("dve", lambda e, bk=bk, hh=hh, h=h: e.scalar_tensor_tensor(
                        out=scrA[0:T, h * 128:(h + 1) * 128], in0=bk[0:T, hh * 128:(hh + 1) * 128], scalar=st[0:T, 8 + h:9 + h],
                        in1=sg_hg[0:T, h * 128:(h + 1) * 128], op0=ALU.mult, op1=ALU.mult),
                       reads=[bb, b_st[1], b_sg[0]], writes=[b_mix[0]])
                yield
            for _ in state_hg_U_g(T, want_bf, cx, True, pool):
                yield

        def ret_gen(T, want_bf, cx):
            pool = BankPool([4, 5])
            bk, bb = nb(pool)
            bkb = bk[:].bitcast(BF16)
            for g in range(4):
                op("pe", lambda e, g=g, bkb=bkb: e.transpose(out=bkb[:, g * T:(g + 1) * T], in_=rqk[0:T, g * 128:(g + 1) * 128],
                                                             identity=ident[0:T, 0:T]), reads=[b_rqk, b_const], writes=[bb])
            yield
            op("act", lambda e, bkb=bkb: e.activation(out=rqkT[:, 2:4, 0:T], in_=bkb[:, 2 * T:4 * T].rearrange("p (g t) -> p g t", t=T), func=AF.Copy),
               reads=[bb], writes=[b_rqkT])
            for h in range(4):
                hb = (h % 2) * 64
                pr = h // 2
                op("act", lambda e, bkb=bkb, h=h, hb=hb, pr=pr: e.activation(out=rqTm[hb:hb + 64, h, 0:T], in_=bkb[hb:hb + 64, pr * T:(pr + 1) * T], func=AF.Copy),
                   reads=[bb], writes=[b_rqkT])
                op("dve", lambda e, bkb=bkb, h=h, hb=hb, pr=pr: e.tensor_tensor(out=rqdTm[hb:hb + 64, h, 0:T], in0=bkb[hb:hb + 64, pr * T:(pr + 1) * T],
                                                                              in1=qdec[hb:hb + 64, pr, 0:T], op=ALU.mult),
                   reads=[bb, b_const], writes=[b_rqkT])
            yield
            bk, bb = nb(pool)
            for h in range(4):
                pr = h // 2
                op("pe", lambda e, bk=bk, h=h, pr=pr: e.matmul(bk[0:T, h * T:(h + 1) * T], rqkT[:, 2 + pr, 0:T],
                                                               rqTm[:, h, 0:T], start=True, stop=True),
                   reads=[b_rqkT], writes=[bb])
            yield
            op("dve", lambda e, bk=bk: e.tensor_tensor(out=PT[0:T, :, 0:T], in0=bk[0:T, 0:4 * T].rearrange("p (h t) -> p h t", t=T),
                                                       in1=Mrt[0:T, :, 0:T], op=ALU.mult), reads=[bb, b_const], writes=[b_PT])
            yield
            bk, bb = nb(pool)
            for h in range(4):
                pr = h // 2
                op("pe", lambda e, bk=bk, h=h: e.matmul(bk[0:T, h * 128:(h + 1) * 128], PT[0:T, h, 0:T], rv_bf[0:T, h * 128:(h + 1) * 128],
                                                        start=True, stop=False), reads=[b_PT, b_rv], writes=[bb])
                op("pe", lambda e, bk=bk, h=h, pr=pr: e.matmul(bk[0:T, h * 128:(h + 1) * 128], rqdTm[:, h, 0:T],
                                                               Rbf[:, pr, :], start=False, stop=True),
                   reads=[b_rqkT, b_Rbf], writes=[bb])
            yield
            for h in range(4):
                op("act", lambda e, bk=bk, h=h: e.activation(out=junk[0:T, 512 + h * 128:512 + (h + 1) * 128], in_=bk[0:T, h * 128:(h + 1) * 128], func=AF.Copy,
                                                             accum_out=st[0:T, 16 + h:17 + h]), reads=[bb], writes=[b_jk[1], b_st[2]])
            yield
            for h in range(4):
                op("act", lambda e, bk=bk, h=h: e.activation(out=junk[0:T, 512 + h * 128:512 + (h + 1) * 128], in_=bk[0:T, h * 128:(h + 1) * 128], func=AF.Square,
                                                             accum_out=st[0:T, 20 + h:21 + h]), reads=[bb], writes=[b_jk[1], b_st[3]])
            yield
            op("dve", lambda e: e.tensor_scalar_mul(out=st[0:T, 16:20], in0=st[0:T, 16:20], scalar1=1.0 / 128), reads=[b_st[2]], writes=[b_st[2]])
            op("dve", lambda e: e.tensor_tensor(out=st[0:T, 24:28], in0=st[0:T, 16:20], in1=st[0:T, 16:20], op=ALU.mult),
               reads=[b_st[2]], writes=[b_st[4]])
            op("dve", lambda e: e.scalar_tensor_tensor(out=st[0:T, 20:24], in0=st[0:T, 20:24], scalar=1.0 / 128, in1=st[0:T, 24:28],
                                                       op0=ALU.mult, op1=ALU.subtract), reads=[b_st[3], b_st[4]], writes=[b_st[3]])
            yield
            op("act", lambda e: e.activation(out=st[0:T, 20:24], in_=st[0:T, 20:24], func=AF.Ln, bias=EPS, scale=1.0), reads=[b_st[3]], writes=[b_st[3]])
            op("act", lambda e: e.activation(out=st[0:T, 20:24], in_=st[0:T, 20:24], func=AF.Exp, scale=-0.5), reads=[b_st[3]], writes=[b_st[3]])
            yield
            for h in range(4):
                op("dve", lambda e, bk=bk, h=h: e.scalar_tensor_tensor(out=tnx[0:T, h * 128:(h + 1) * 128], in0=bk[0:T, h * 128:(h + 1) * 128],
                                                                       scalar=st[0:T, 16 + h:17 + h], in1=sg_rt[0:T, h * 128:(h + 1) * 128],
                                                                       op0=ALU.subtract, op1=ALU.mult),
                   reads=[bb, b_st[2], b_sg[1]], writes=[b_tnx])
                op("pool", lambda e, h=h: e.tensor_scalar(out=scrA[0:T, 1024 + h * 128:1024 + (h + 1) * 128], in0=tnx[0:T, h * 128:(h + 1) * 128],
                                                          scalar1=st[0:T, 20 + h:21 + h], scalar2=1.0, op0=ALU.mult, op1=ALU.mult),
                   reads=[b_tnx, b_st[3]], writes=[b_mix[1]])
                yield
            state_update_rt(T, want_bf, cx, pool)
            yield

        def xa_gen(T):
            pool = BankPool([6, 7])
            xb = []
            for hp in range(2):
                bk, bb = nb(pool)
                for hh in range(2):
                    h = hp * 2 + hh
                    for mc in range(2):
                        op("pe", lambda e, bk=bk, hh=hh, h=h, mc=mc: e.matmul(bk[:, (hh * 2 + mc) * T:(hh * 2 + mc + 1) * T], mkT[:, mc, h, :],
                                                                              xqT[:, h, 0:T], start=True, stop=True),
                           reads=[b_mkT, b_xqT], writes=[bb])
                xb.append((bk, bb))
                yield
            for hp in range(2):
                bk, bb = xb[hp]
                op("act", lambda e, bk=bk, hp=hp: e.activation(out=xP[:, hp * 4:hp * 4 + 4, 0:T],
                                                               in_=bk[:, 0:4 * T].rearrange("p (a t) -> p a t", t=T), func=AF.Exp),
                   reads=[bb], writes=[b_xP[hp]])
                yield
            for hp in range(2):
                bk, bb = nb(pool)
                for hh in range(2):
                    h = hp * 2 + hh
                    for mc in range(2):
                        op("pe", lambda e, bk=bk, hh=hh, h=h, mc=mc, hp=hp: e.matmul(bk[0:T, hh * 130:(hh + 1) * 130], xP[:, hp * 4 + hh * 2 + mc, 0:T],
                                                                                      mvA[:, mc, h, :], start=(mc == 0), stop=(mc == 1)),
                           reads=[b_xP[hp], b_mvA], writes=[bb])
                yield
                bk3 = bk[0:T, 0:260].rearrange("p (h c) -> p h c", c=130)
                op("dve", lambda e, bk3=bk3, hp=hp: e.reciprocal(out=st[0:T, 28 + hp * 2:30 + hp * 2], in_=bk3[:, :, 128]),
                   reads=[bb], writes=[b_st[5]])
                for hh in range(2):
                    h = hp * 2 + hh
                    op("dve", lambda e, bk=bk, hh=hh, h=h: e.scalar_tensor_tensor(
                        out=scrA[0:T, 1536 + h * 128:1536 + (h + 1) * 128], in0=bk[0:T, hh * 130:hh * 130 + 128],
                        scalar=st[0:T, 28 + h:29 + h], in1=sg_xa[0:T, h * 128:(h + 1) * 128], op0=ALU.mult, op1=ALU.mult),
                       reads=[bb, b_st[5], b_sg[2]], writes=[b_mix[2]])
                yield

        def front_gen(T, slot, next_load, delay):
            for _ in range(delay):
                yield
            xx, bx = xt[slot], b_xt[slot]
            rms_rstd(xx[0:T, :], bx, T, 0, 1.0 / D)
            yield
            op("act", lambda e: e.activation(out=scrA[0:T, 0:D], in_=xx[0:T, :], func=AF.Copy, scale=st[0:T, 0:1]),
               reads=[bx, b_st[0]], writes=[b_scrA, b_mix[0]])
            yield
            bk, bb = nb(BankPool([4]))
            bkb = bk[:].bitcast(BF16)
            for kc in range(8):
                op("pe", lambda e, kc=kc, bkb=bkb: e.transpose(out=bkb[:, kc * T:(kc + 1) * T], in_=scrA[0:T, kc * 128:(kc + 1) * 128],
                                                               identity=ident[0:T, 0:T]), reads=[b_scrA, b_mix[0], b_const], writes=[bb])
            yield
            op("act", lambda e, bkb=bkb: e.activation(out=hk[:, :, 0:T], in_=bkb[:, 0:8 * T].rearrange("p (k t) -> p k t", t=T), func=AF.Copy),
               reads=[bb], writes=[b_hk])
            if next_load is not None:
                next_load()
            yield

        def outproj_gen(T, slot, y_dram_rows):
            xx, bx = xt[slot], b_xt[slot]
            pool = BankPool([0, 1, 2, 3])
            for hf in range(2):
                bk, bb = nb(pool)
                bkb = bk[:].bitcast(BF16)
                for ff in range(8):
                    f = hf * 8 + ff
                    op("pe", lambda e, bkb=bkb, ff=ff, f=f: e.transpose(out=bkb[:, ff * T:(ff + 1) * T], in_=scrA[0:T, f * 128:(f + 1) * 128],
                                                                        identity=ident[0:T, 0:T]), reads=b_mix + [b_const], writes=[bb])
                if hf == 0:
                    op("act", lambda e, bkb=bkb, hf=hf: e.activation(out=qk[:, hf * 8:hf * 8 + 8, 0:T],
                                                                     in_=bkb[:, 0:8 * T].rearrange("p (f t) -> p f t", t=T), func=AF.Copy),
                       reads=[bb], writes=[b_qk, b_qk2])
                else:
                    op("dve", lambda e, bkb=bkb, hf=hf: e.tensor_copy(out=qk[:, hf * 8:hf * 8 + 8, 0:T],
                                                                      in_=bkb[:, 0:8 * T].rearrange("p (f t) -> p f t", t=T)),
                       reads=[bb], writes=[b_qk, b_qk2])
                yield
            for nbk in range(2):
                bk, bb = nb(pool)
                for f in range(16):
                    op("pe", lambda e, bk=bk, f=f, nbk=nbk: e.matmul(bk[0:T, :], qk[:, f, 0:T], wout3[:, f, nbk * 512:(nbk + 1) * 512],
                                                                      start=(f == 0), stop=(f == 15)), reads=[b_qk, b_wout[f]], writes=[bb])
                op("dve", lambda e, bk=bk, nbk=nbk: e.tensor_tensor(out=xx[0:T, nbk * 512:(nbk + 1) * 512], in0=bk[0:T, :],
                                                                    in1=xx[0:T, nbk * 512:(nbk + 1) * 512], op=ALU.add),
                   reads=[bb, bx], writes=[bx])
                yield
            sc = st[0:T, 1:2]
            op("act", lambda e: e.activation(out=junk[0:T, 0:D], in_=xx[0:T, :], func=AF.Square, accum_out=sc),
               reads=[bx], writes=[b_jk[0], b_jk[1], b_st[6]])
            op("act", lambda e: e.activation(out=sc, in_=sc, func=AF.Ln, bias=EPS, scale=1.0 / D), reads=[b_st[6]], writes=[b_st[6]])
            op("act", lambda e: e.activation(out=sc, in_=sc, func=AF.Exp, scale=-0.5), reads=[b_st[6]], writes=[b_st[6]])
            yield
            op("dve", lambda e: e.scalar_tensor_tensor(out=ot[0:T, :], in0=xx[0:T, :], scalar=st[0:T, 1:2], in1=gfin_sb[0:T, :],
                                                       op0=ALU.mult, op1=ALU.mult), reads=[bx, b_st[6], b_gfin], writes=b_otl)
            op("sp", lambda e: e.dma_start(out=y_dram_rows, in_=ot[0:T, :]), reads=b_otl)
            yield

        def tile(T, tabi, tabbuf, y_dram_rows, slot, want_bf, nxt=None, extra=None):
            cx = cx0
            run_interleaved(fq_par_gens(T, cx) + [tok_gen(T, cx, tabi, tabbuf, (4, 5, 6, 7))] + ([extra] if extra is not None else []))
            run_interleaved([hgrn_gen(T, want_bf, cx), ret_gen(T, want_bf, cx), xa_gen(T)])
            gens = [outproj_gen(T, slot, y_dram_rows)]
            if nxt is not None:
                gens.append(front_gen(nxt[0], nxt[1], nxt[2], 2))
            run_interleaved(gens)

        op("pool", lambda e: e.memset(S_sb[:], 0.0), writes=b_S)
        op("pool", lambda e: e.memset(R_sb[:], 0.0), writes=[b_R])
        op("pool", lambda e: e.memset(Sbf[:], 0.0), writes=[b_Sbf])
        op("pool", lambda e: e.memset(Rbf[:], 0.0), writes=[b_Rbf])
        op("pool", lambda e: e.memset(rqTm[:], 0.0), writes=[b_rqkT])
        op("pool", lambda e: e.memset(rqdTm[:], 0.0), writes=[b_rqkT])
        op("pool", lambda e: e.memset(mvA[:], 0.0), writes=[b_mvA])
        op("pool", lambda e: e.memset(mvA[:, :, :, 128:129], 1.0), writes=[b_mvA])

        KSTOP = 9
        njobs = len(bg_jobs)
        per = (njobs + max(NPRE, 1) - 1) // max(NPRE, 1)
        ji = 0
        cur = 0
        def run_interleaved(gens):
            gens = list(gens)
            while gens:
                for g in list(gens):
                    try:
                        next(g)
                    except StopIteration:
                        gens.remove(g)

        def pre_f_gen(hf, cx, Bu, bBu, Bg, bBg, pool, hT, bhT):
            T = 128
            n = 512
            bk, bb = fm_group(C_F + hf * 512, T, pool, hT, bhT)
            yield
            op("act", lambda e: e.activation(out=Bu[:, :], in_=bk[:, :], func=AF.Exp, scale=-1.0), reads=[bb], writes=[bBu])
            yield
            for hh in range(4):
                h = hf * 4 + hh
                op("act", lambda e, hh=hh, h=h: e.activation(out=Bg[:, hh * T:(hh + 1) * T], in_=Bu[:, hh * T:(hh + 1) * T], func=AF.Ln,
                                                             bias=1.0, scale=lbc[:, 24 + h:25 + h]),
                   reads=[bBu, b_lbc], writes=[bBg])
            yield
            op("act", lambda e: e.activation(out=Bu[:, :], in_=Bu[:, :], func=AF.Ln, bias=1.0, scale=1.0), reads=[bBu], writes=[bBu])
            yield
            op("dve", lambda e: e.tensor_tensor(out=Bg[:, :], in0=Bg[:, :], in1=Bu[:, :], op=ALU.subtract), reads=[bBg, bBu], writes=[bBg])
            yield
            for hh in range(4):
                op("dve", lambda e, hh=hh: e.tensor_tensor_scan(out=Bu[:, hh * T:(hh + 1) * T], data0=onesf[:, 0:T],
                                                                data1=Bg[:, hh * T:(hh + 1) * T], initial=0.0,
                                                                op0=ALU.mult, op1=ALU.add),
                   reads=[bBg, b_const], writes=[bBu])
                yield
            gc3 = Bu[:, :].rearrange("p (h t) -> p h t", t=T)
            e0 = cx["E"] + hf * 4
            op("act", lambda e: e.activation(out=Et[:, e0:e0 + 4], in_=gc3[:, :, T - 1], func=AF.Exp), reads=[bBu], writes=[cx["bE"]])
            op("act", lambda e: e.activation(out=Bg[:, :], in_=Bu[:, :], func=AF.Exp, scale=-1.0), reads=[bBu], writes=[bBg])
            yield
            em3 = Bg[:, :].rearrange("p (h t) -> p h t", t=T)
            kd = cx["k"][:, hf * 4:hf * 4 + 4, :]
            op("dve", lambda e: e.tensor_tensor(out=kd[:, :, 1:T], in0=em3[:, :, 1:T], in1=em3[:, :, 0:T - 1], op=ALU.subtract),
               reads=[bBg], writes=[cx["bk"]])
            op("dve", lambda e: e.tensor_scalar_add(out=kd[:, :, 0], in0=em3[:, :, 0], scalar1=-1.0), reads=[bBg], writes=[cx["bk"]])
            yield

        def pre_tok_gen(cx, cur_, pool, hT, bhT):
            for half in range(2):
                bk, bb = tm_group(C_I + half * 512, 512, 128, 0, pool, hT, bhT)
                op("act", lambda e, bk=bk, half=half: e.activation(out=cx["v"][:, half * 512:(half + 1) * 512], in_=bk[:, :], func=AF.Copy),
                   reads=[bb], writes=[cx["bv"]])
                yield
            bk, bb = tm_group(C_RK, 256, 128, 256, pool, hT, bhT)
            yield
            rope(bk, bb, 128, 4, 8, 2 * cur_, b_rtab[cur_])
            yield
            make_rkd(128, cx)
            bk, bb = tm_group(C_RV, 512, 128, 0, pool, hT, bhT)
            yield
            op("act", lambda e, bk=bk: e.activation(out=cx["rv"][:, :], in_=bk[:, :], func=AF.Copy), reads=[bb], writes=[cx["brv"]])
            yield
            rope_advance(cur_)
            yield

        def pre_B_gen(want_bf, cx):
            pool = BankPool([4, 5, 6])
            state_hg_T(128, cx, pool)
            yield
            for _ in state_hg_U_g(128, want_bf, cx, True, pool):
                yield
            state_update_rt(128, want_bf, cx, pool)
            yield

        def pre_front_gen(i, next_load, delay, delay2=0):
            for _ in range(delay):
                yield
            slot = i % 2
            rms_rstd(xt[slot][:, :], b_xt[slot], 128, 0, 1.0 / D)
            yield
            op("pool", lambda e: e.tensor_scalar(out=scrA[:, 0:D], in0=xt[slot][:, :], scalar1=st[:, 0:1], scalar2=1.0, op0=ALU.mult, op1=ALU.mult),
               reads=[b_xt[slot], b_st[0]], writes=[b_scrA])
            yield
            for _ in range(delay2):
                yield
            bk, bb = nb(BankPool([7]))
            bkb = bk[:].bitcast(BF16)
            for kc in range(8):
                op("pe", lambda e, kc=kc: e.transpose(out=bkb[:, kc * 128:(kc + 1) * 128], in_=scrA[:, kc * 128:(kc + 1) * 128],
                                                      identity=ident[:, :]), reads=[b_scrA, b_const], writes=[bb])
            yield
            hTd, bhTd = (hk, b_hk) if i % 2 == 0 else (hk2, b_hk2)
            op("act", lambda e: e.activation(out=hTd[:, :, :], in_=bkb[:, :].rearrange("p (k t) -> p k t", t=128), func=AF.Copy),
               reads=[bb], writes=[bhTd])
            if next_load is not None:
                next_load()
            yield

        def pre_A_gens(i, cur_):
            cx = cx0p if i % 2 == 0 else cx1
            hT, bhT = (hk, b_hk) if i % 2 == 0 else (hk2, b_hk2)
            return [pre_f_gen(0, cx, Bt[0], b_Bt[0], Bt[1], b_Bt[1], BankPool([0]), hT, bhT),
                    pre_f_gen(1, cx, Bt[2], b_Bt[2], Bx, b_Bx, BankPool([1]), hT, bhT),
                    pre_tok_gen(cx, cur_, BankPool([2, 3]), hT, bhT)]

        x_extra[0] = b_sx[0:2]
        x_extra[1] = b_sx[2:4]
        load_x(xp_d[0:128, :], 128, 0)
        if NPRE > 0:
            run_interleaved([pre_front_gen(0, (lambda: load_x(xp_d[128:256, :], 128, 1)) if NPRE > 1 else None, 0)])
        for t in range(NPRE):
            gens = pre_A_gens(t, cur)
            cur = 1 - cur
            if t >= 1:
                gens.append(pre_B_gen(False, cx0p if (t - 1) % 2 == 0 else cx1))
            if t + 1 < NPRE:
                nl = (lambda t=t: load_x(xp_d[(t + 2) * 128:(t + 3) * 128, :], 128, t % 2)) if t + 2 < NPRE else None
                gens.insert(0, pre_front_gen(t + 1, nl, 1, 3))
            run_interleaved(gens)
            for _ in range(per):
                if ji < njobs:
                    bg_jobs[ji]()
                    ji += 1
        if NPRE > 0:
            run_interleaved([pre_B_gen(True, cx0p if (NPRE - 1) % 2 == 0 else cx1)])
        while ji < njobs and KSTOP >= 3:
            bg_jobs[ji]()
            ji += 1

        for mc in range(2 if KSTOP >= 4 else 0):
            slot = mc % 2
            op("sp", lambda e, mc=mc, slot=slot: e.dma_start(out=xt[slot][:, :], in_=mem_d[mc * 128:(mc + 1) * 128, :]), writes=[b_xt[slot]])
            norm_transpose(xt[slot][:, :], b_xt[slot], 128)
            bk, bb = nb()
            for h in range(4):
                for kc in range(8):
                    op("pe", lambda e, bk=bk, h=h, kc=kc: e.matmul(bk[:, h * 128:(h + 1) * 128], wmem4[:, 0, kc, h * 128:(h + 1) * 128],
                                                                   hk[:, kc, :], start=(kc == 0), stop=(kc == 7)),
                       reads=[b_hk, b_wmem[0]], writes=[bb])
            op("act", lambda e, bk=bk, mc=mc: e.activation(out=mkT[:, mc, :, :], in_=bk[:, :].rearrange("p (h m) -> p h m", m=128), func=AF.Copy),
               reads=[bb], writes=[b_mkT])
            for w, od in ((0, mkp_d), (1, mvp_d)):
                bk, bb = nb()
                for kc in range(8):
                    op("pe", lambda e, bk=bk, w=w, kc=kc: e.matmul(bk[:, :], hk[:, kc, :], wmem4[:, w, kc, :], start=(kc == 0), stop=(kc == 7)),
                       reads=[b_hk, b_wmem[w]], writes=[bb])
                op("act", lambda e, bk=bk: e.activation(out=ot[:, 0:512], in_=bk[:, :], func=AF.Copy), reads=[bb], writes=b_otl)
                if w == 1:
                    op("dve", lambda e, bk=bk, mc=mc: e.tensor_copy(out=mvA[:, mc, :, 0:128], in_=bk[:, :].rearrange("p (h v) -> p h v", v=128)),
                       reads=[bb], writes=[b_mvA])
                op("sp", lambda e, od=od, mc=mc: e.dma_start(out=od[mc * 128:(mc + 1) * 128, :], in_=ot[:, 0:512]), reads=b_otl)
        def wout_gen():
            cast_engs[:] = ["pool", "dve", "act"]
            for j in wout_jobs:
                j()
                yield
            cast_engs[:] = ["pool"]

        s0slot = (NPRE + NMAIN) % 2
        if KSTOP >= 5:
            load_x(xp_d[NPRE * 128:(NPRE + 1) * 128, :], 128, NPRE % 2)
            nl0 = (lambda: load_x(xp_d[(NPRE + 1) * 128:(NPRE + 2) * 128, :], 128, (NPRE + 1) % 2)) if NMAIN > 1 else \
                  (lambda: load_x(xs_d[0:64, :], 64, s0slot))
            run_interleaved([front_gen(128, NPRE % 2, nl0, 0)])
        for t in range(NMAIN if KSTOP >= 5 else 0):
            tt = NPRE + t
            if t < NMAIN - 2:
                nxt = (128, (tt + 1) % 2, (lambda tt=tt: load_x(xp_d[(tt + 2) * 128:(tt + 3) * 128, :], 128, tt % 2)))
            elif t == NMAIN - 2:
                nxt = (128, (tt + 1) % 2, (lambda: load_x(xs_d[0:64, :], 64, s0slot)))
            else:
                nxt = None
            tile(128, 2 * cur, b_rtab[cur], yp_d[t * 128:(t + 1) * 128, :], tt % 2, True, nxt, wout_gen() if t == 0 else None)
            if t < NMAIN - 1:
                rope_advance(cur)
                cur = 1 - cur
        op("sp", lambda e: e.dma_start(out=shgp_d.rearrange("h d v -> d h v"), in_=S_sb[:]), reads=b_S)
        op("sp", lambda e: e.dma_start(out=srtp_d.rearrange("(p hh) d v -> (hh d) p v", hh=2), in_=R_sb[:]), reads=[b_R])

        def cacheprep_gen(b, stg, bstg):
            stk = stg[:, :].rearrange("p (c n) -> p c n", n=512)
            op("sp", lambda e: e.dma_start(out=stk, in_=ck_d[b].rearrange("(c p) n -> p c n", p=128)), writes=[bstg])
            yield
            op("dve", lambda e: e.tensor_copy(out=junk[:, :], in_=stg[:, :]), reads=[bstg], writes=[b_junk])
            op("sp", lambda e: e.dma_start(out=stk, in_=cv_d[b].rearrange("(c p) n -> p c n", p=128)), writes=[bstg])
            yield
            bk, bb = nb(BankPool([7]))
            bkb = bk[:].bitcast(BF16)
            for mc in range(2):
                for h in range(4):
                    op("pe", lambda e, bkb=bkb, mc=mc, h=h: e.transpose(out=bkb[:, (mc * 4 + h) * 128:(mc * 4 + h + 1) * 128],
                                                                        in_=junk[:, mc * 512 + h * 128:mc * 512 + (h + 1) * 128], identity=ident[:, :]),
                       reads=[b_junk, b_const], writes=[bb])
            yield
            op("act", lambda e, bkb=bkb: e.activation(out=mkT[:].rearrange("p c h m -> p (c h m)"), in_=bkb[:, :], func=AF.Copy),
               reads=[bb], writes=[b_mkT])
            yield
            op("act", lambda e: e.activation(out=mvA[:, :, :, 0:128], in_=stg[:, :].rearrange("p (c h v) -> p c h v", c=2, h=4), func=AF.Copy),
               reads=[bstg], writes=[b_mvA])
            yield

        for b in range(NSAMP if KSTOP >= 6 else 0):
            slot = (s0slot + b) % 2
            op("sp", lambda e, b=b: e.dma_start(out=S_sb[:], in_=sthg_d[b].rearrange("h d v -> d h v")), writes=b_S)
            op("sp", lambda e, b=b: e.dma_start(out=R_sb[:], in_=strt_d[b].rearrange("(p hh) d v -> (hh d) p v", hh=2)), writes=[b_R])
            op("act", lambda e: e.activation(out=Sbf[:], in_=S_sb[:], func=AF.Copy), reads=b_S, writes=[b_Sbf])
            op("dve", lambda e: e.tensor_copy(out=Rbf[:], in_=R_sb[:]), reads=[b_R], writes=[b_Rbf])
            if b == 0:
                run_interleaved([front_gen(64, slot, None, 0)])
            T = 64
            run_interleaved([cacheprep_gen(b, xt[1 - slot], b_xt[1 - slot])] + fq_par_gens(T, cx0) + [tok_gen(T, cx0, 4, b_rtab[2], (4, 5, 6, 7))])
            if b + 1 < NSAMP:
                load_x(xs_d[(b + 1) * 64:(b + 2) * 64, :], 64, 1 - slot)
            run_interleaved([hgrn_gen(T, False, cx0), ret_gen(T, False, cx0), xa_gen(T)])
            gens = [outproj_gen(T, slot, ys_d[b * 64:(b + 1) * 64, :])]
            if b + 1 < NSAMP:
                gens.append(front_gen(64, 1 - slot, None, 2))
            run_interleaved(gens)
            op("sp", lambda e, b=b: e.dma_start(out=shgs_d[b].rearrange("h d v -> d h v"), in_=S_sb[:]), reads=b_S)
            op("sp", lambda e, b=b: e.dma_start(out=srts_d[b].rearrange("(p hh) d v -> (hh d) p v", hh=2), in_=R_sb[:]), reads=[b_R])

        P.finish()
        with nc.Block() as block:
            P.emit(block)
    return nc


_CACHE = {}


def kernel(x_prompt, x_sample, mem_prompt, state_hgrn, state_ret, cache_mem_k, cache_mem_v,
           norm_g, w_in, lb_logits, hg_norm_g, rt_norm_g, mem_norm_g, w_mem_k, w_mem_v, w_out, final_norm_g):
    f = np.float32
    x_prompt = np.asarray(x_prompt, f)
    x_sample = np.asarray(x_sample, f)
    B, SEQ, _ = x_prompt.shape
    DB = x_sample.shape[0]
    assert B == 2 and SEQ % 512 == 0 and DB % NCORES == 0 and x_sample.shape[1] == 64
    L = SEQ // 4
    NMAIN = L // 128
    NPRE = 3 * NMAIN
    NSAMP = DB // NCORES
    key = (NPRE, NMAIN, NSAMP)
    if key not in _CACHE:
        _CACHE[key] = build(*key)
    nc = _CACHE[key]

    def lay(v, k):
        return np.ascontiguousarray(np.asarray(v, f).reshape(k, 128).T)

    ng = lay(norm_g[0], 8)
    mg = lay(mem_norm_g[0], 8)
    gout = np.concatenate([lay(hg_norm_g[0], 8), lay(rt_norm_g[0], 4)], axis=1)
    lbl = np.concatenate([lay(lb_logits[0], 8), lay(lb_logits[1], 8)], axis=1)
    gfin = np.ascontiguousarray(np.broadcast_to(np.asarray(final_norm_g, f)[None, :], (128, D)))
    win = np.ascontiguousarray(np.asarray(w_in[0], f))
    wout = np.ascontiguousarray(np.asarray(w_out[0], f))
    wmk = np.ascontiguousarray(np.asarray(w_mem_k[0], f))
    wmv = np.ascontiguousarray(np.asarray(w_mem_v[0], f))
    in_maps = []
    for c in range(NCORES):
        s, j = c // 4, c % 4
        end = (j + 1) * L
        xp = np.zeros(((NPRE + NMAIN) * 128, D), f)
        xp[(NPRE + NMAIN) * 128 - end:, :] = x_prompt[s, :end, :]
        posb = np.zeros((128, 2), f)
        posb[:, 0] = np.arange(128, dtype=f) + f(end - (NPRE + NMAIN) * 128)
        posb[:, 1] = np.arange(128, dtype=f) + f(PAST_LEN)
        sl = slice(c * NSAMP, (c + 1) * NSAMP)
        in_maps.append({
            "xp": xp,
            "xs": np.ascontiguousarray(x_sample[sl].reshape(NSAMP * 64, D)),
            "mem": np.ascontiguousarray(np.asarray(mem_prompt[s], f)),
            "sthg": np.ascontiguousarray(np.asarray(state_hgrn[0, sl], f)),
            "strt": np.ascontiguousarray(np.asarray(state_ret[0, sl], f)),
            "ck": np.ascontiguousarray(np.asarray(cache_mem_k[0, sl], f).reshape(NSAMP, NMEM, 512)),
            "cv": np.ascontiguousarray(np.asarray(cache_mem_v[0, sl], f).reshape(NSAMP, NMEM, 512)),
            "win": win, "wout": wout, "wmk": wmk, "wmv": wmv,
            "ng": ng, "mg": mg, "gout": gout, "lbl": lbl, "gfin": gfin, "posb": posb,
        })
    res = run_bass_kernel_spmd(nc, in_maps, core_ids=list(range(NCORES))).results

    y_prompt = np.zeros((B, SEQ, D), f)
    y_sample = np.zeros((DB, 64, D), f)
    shg_p = np.zeros((1, B, 8, 128, 128), f)
    srt_p = np.zeros((1, B, 4, 64, 128), f)
    mk_p = np.zeros((1, B, NMEM, 4, 128), f)
    mv_p = np.zeros((1, B, NMEM, 4, 128), f)
    shg_s = np.zeros((1, DB, 8, 128, 128), f)
    srt_s = np.zeros((1, DB, 4, 64, 128), f)
    for c in range(NCORES):
        s, j = c // 4, c % 4
        r = res[c]
        y_prompt[s, j * L:(j + 1) * L] = r["yp"]
        sl = slice(c * NSAMP, (c + 1) * NSAMP)
        y_sample[sl] = r["ys"].reshape(NSAMP, 64, D)
        shg_s[0, sl] = r["shgs"]
        srt_s[0, sl] = r["srts"]
        if j == 3:
            shg_p[0, s] = r["shgp"]
            srt_p[0, s] = r["srtp"]
            mk_p[0, s] = r["mkp"].reshape(NMEM, 4, 128)
            mv_p[0, s] = r["mvp"].reshape(NMEM, 4, 128)
    return (y_prompt, y_sample, shg_p, srt_p, mk_p, mv_p, shg_s, srt_s)
```

```python
import math
from contextlib import ExitStack

import numpy as np
import concourse.bass as bass
import concourse.mybir as mybir
from concourse.bass_utils import run_bass_kernel_spmd

F32 = mybir.dt.float32
BF16 = mybir.dt.bfloat16
I32 = mybir.dt.int32
AF = mybir.ActivationFunctionType
ALU = mybir.AluOpType
AX = mybir.AxisListType

D = 1024
DIN = 6656
DMIX = 2048
NMEM = 256
C_Q, C_F, C_I, C_G, C_RQ, C_RK, C_RV, C_RG, C_XQ, C_XG = 0, 1024, 2048, 3072, 4096, 4352, 4608, 5120, 5632, 6144
EPS = 1e-6
PAST_LEN = 4096
LG = [math.log1p(-(2.0 ** (-5 - h))) for h in range(4)]
NCORES = 8
NDS = 12
TWO_PI = 2.0 * math.pi
CW1 = 6.28125
CW2 = TWO_PI - CW1


class Sem:
    def __init__(self, h, step):
        self.h = h
        self.step = step
        self.count = 0


class Buf:
    __slots__ = ("w", "r")

    def __init__(self):
        self.w = None
        self.r = {}


class Prog:
    ENG = ("pe", "act", "dve", "pool", "sp")

    def __init__(self, nc, es):
        self.nc = nc
        self.q = {k: [] for k in self.ENG}
        self.esem = {k: Sem(es.enter_context(nc.semaphore("sem_" + k)), 1) for k in ("pe", "act", "dve", "pool")}
        self.dsems = [Sem(es.enter_context(nc.semaphore("dsem%d" % i)), 16) for i in range(NDS)]
        self.di = 0
        self.waited = {k: {} for k in self.ENG}
        self.nops = 0

    def op(self, eng, fn, reads=(), writes=()):
        deps = {}

        def add(sem, c):
            if deps.get(sem, 0) < c:
                deps[sem] = c

        for b in reads:
            if b.w is not None:
                add(*b.w)
        for b in writes:
            if b.w is not None:
                add(*b.w)
            for sem, c in b.r.items():
                add(sem, c)
        if eng == "sp":
            mysem = self.dsems[self.di % len(self.dsems)]
            self.di += 1
            if mysem.count > 0:
                add(mysem, mysem.count)
        else:
            mysem = self.esem[eng]
        w = self.waited[eng]
        for sem, c in deps.items():
            if eng == "pe" and sem is self.esem["pe"]:
                continue
            if w.get(sem, 0) >= c:
                continue
            w[sem] = c
            self.q[eng].append(("w", sem.h, c))
        mysem.count += mysem.step
        c = mysem.count
        self.q[eng].append(("i", fn, mysem.h, mysem.step))
        self.nops += 1
        for b in reads:
            if b.r.get(mysem, 0) < c:
                b.r[mysem] = c
        for b in writes:
            b.w = (mysem, c)
            b.r = {}

    def finish(self):
        for sem in self.dsems:
            if sem.count > 0:
                self.q["sp"].append(("w", sem.h, sem.count))

    def emit(self, block):
        def mk(name):
            def f(e):
                for it in self.q[name]:
                    if it[0] == "w":
                        e.wait_ge(it[1], it[2])
                    else:
                        it[1](e).then_inc(it[2], it[3])
            return f

        block.tensor(mk("pe"))
        block.scalar(mk("act"))
        block.vector(mk("dve"))
        block.gpsimd(mk("pool"))
        block.sync(mk("sp"))


def build(NPRE, NMAIN, NSAMP):
    NTRAV = NPRE + NMAIN
    nc = bass.Bass("TRN2", target_bir_lowering=False)

    def din(name, shape):
        return nc.dram_tensor(name, list(shape), F32, kind="ExternalInput").ap()

    def dout(name, shape):
        return nc.dram_tensor(name, list(shape), F32, kind="ExternalOutput").ap()

    xp_d = din("xp", [NTRAV * 128, D])
    xs_d = din("xs", [NSAMP * 64, D])
    mem_d = din("mem", [NMEM, D])
    sthg_d = din("sthg", [NSAMP, 8, 128, 128])
    strt_d = din("strt", [NSAMP, 4, 64, 128])
    ck_d = din("ck", [NSAMP, NMEM, 512])
    cv_d = din("cv", [NSAMP, NMEM, 512])
    win_d = din("win", [D, DIN])
    wout_d = din("wout", [DMIX, D])
    wmk_d = din("wmk", [D, 512])
    wmv_d = din("wmv", [D, 512])
    ng_d = din("ng", [128, 8])
    mg_d = din("mg", [128, 8])
    gout_d = din("gout", [128, 12])
    lbl_d = din("lbl", [128, 16])
    gfin_d = din("gfin", [128, D])
    posb_d = din("posb", [128, 2])

    yp_d = dout("yp", [NMAIN * 128, D])
    ys_d = dout("ys", [NSAMP * 64, D])
    shgp_d = dout("shgp", [8, 128, 128])
    srtp_d = dout("srtp", [4, 64, 128])
    mkp_d = dout("mkp", [NMEM, 512])
    mvp_d = dout("mvp", [NMEM, 512])
    shgs_d = dout("shgs", [NSAMP, 8, 128, 128])
    srts_d = dout("srts", [NSAMP, 4, 64, 128])

    with ExitStack() as es:
        P = Prog(nc, es)
        op = P.op

        def sb(name, shape, dt):
            return es.enter_context(nc.sbuf_tensor(name, list(shape), dt))

        win_sb = sb("win_sb", [128, 8, DIN], BF16)
        wout_sb = sb("wout_sb", [128, 16 * D], BF16)
        wout3 = wout_sb[:].rearrange("p (k n) -> p k n", n=D)
        wmem4 = wout_sb[:, 0:8192].rearrange("p (w k n) -> p w k n", w=2, k=8)
        Bx = wout_sb[:, 8192:9216].bitcast(F32)
        b_Bx = Buf()
        hk2 = wout_sb[:, 9216:10240].rearrange("p (k t) -> p k t", t=128)
        b_hk2 = Buf()
        b_win = [[Buf() for _ in range(13)] for _ in range(8)]
        b_wout = [Buf() for _ in range(16)]
        b_wmem = [Buf(), Buf()]

        ng_sb = sb("ng_sb", [128, 8], F32)
        mg_sb = sb("mg_sb", [128, 8], F32)
        gout_sb = sb("gout_sb", [128, 12], F32)
        lbl_sb = sb("lbl_sb", [128, 16], F32)
        posb_sb = sb("posb_sb", [128, 2], F32)
        gfin_sb = sb("gfin_sb", [128, D], F32)
        b_small = Buf()
        b_gfin = Buf()
        lbc = sb("lbc", [128, 32], F32)
        b_lbc = Buf()

        ident = sb("ident", [128, 128], BF16)
        onesf = sb("onesf", [128, 128], F32)
        Mc = sb("Mc", [128, 128], F32)
        Mrt = sb("Mrt", [128, 4, 128], F32)
        qdec = sb("qdec", [128, 2, 128], F32)
        kdec = sb("kdec", [128, 8], F32)
        b_const = Buf()

        rtab = sb("rtab", [128, 8, 32], F32)
        b_rtab = [Buf() for _ in range(4)]
        rtmp = sb("rtmp", [128, 6, 32], F32)
        b_rtmp = Buf()
        rti = sb("rti", [128, 32], I32)

        S_sb = sb("S_sb", [128, 8, 128], F32)
        Sbf = sb("Sbf", [128, 8, 128], BF16)
        R_sb = sb("R_sb", [128, 2, 128], F32)
        Rbf = sb("Rbf", [128, 2, 128], BF16)
        b_S = [Buf() for _ in range(8)]
        b_Sbf = Buf()
        b_R = Buf()
        b_Rbf = Buf()

        mkT = sb("mkT", [128, 2, 4, 128], BF16)
        mvA = sb("mvA", [128, 2, 4, 130], BF16)
        b_mkT = Buf()
        b_mvA = Buf()

        stage = [sb("stage%d" % i, [128, 512], F32) for i in range(2)]
        b_stage = [Buf() for _ in range(2)]
        NSTG = 2

        xt = [sb("xt%d" % i, [128, D], F32) for i in range(2)]
        b_xt = [Buf(), Buf()]
        scrA = sb("scrA", [128, DMIX], BF16)
        b_scrA = Buf()
        hk = sb("hk", [128, 8, 128], BF16)
        b_hk = Buf()
        BtAll = sb("BtAll", [128, 1536], F32)
        Bt = [BtAll[:, i * 512:(i + 1) * 512] for i in range(3)]
        b_Bt = [Buf(), Buf(), Buf()]
        b_otl = [b_Bt[1], b_Bt[2]]
        ot = BtAll[:, 512:1536]
        qk = sb("qk", [128, 16, 128], BF16)
        b_qk = Buf()
        xqT = sb("xqT", [128, 4, 128], BF16)
        b_xqT = Buf()
        v_bf = sb("v_bf", [128, D], BF16)
        b_v = Buf()
        sg_hg = sb("sg_hg", [128, D], BF16)
        sg_rt = sb("sg_rt", [128, 512], BF16)
        sg_xa = sb("sg_xa", [128, 512], BF16)
        b_sg = [Buf(), Buf(), Buf()]
        rv_bf = sb("rv_bf", [128, 512], BF16)
        b_rv = Buf()
        rop = sb("rop", [128, 2, 256], F32)
        b_rop = Buf()
        rqk = sb("rqk", [128, 512], BF16)
        b_rqk = Buf()
        rkd = sb("rkd", [128, 256], BF16)
        b_rkd = Buf()
        rqkT = sb("rqkT", [128, 4, 128], BF16)
        rqTm = sb("rqTm", [128, 4, 128], BF16)
        rqdTm = sb("rqdTm", [128, 4, 128], BF16)
        b_rqkT = Buf()
        AT = sb("AT", [128, 8, 128], BF16)
        b_AT = [Buf(), Buf()]
        PT = sb("PT", [128, 4, 128], BF16)
        b_PT = Buf()
        xP = sb("xP", [128, 8, 128], BF16)
        b_xP = [Buf(), Buf()]
        tn = sb("tn", [128, 512], F32)
        b_tn = Buf()
        junk = tn[:].bitcast(BF16)
        b_junk = b_tn
        b_jk = [b_tn, b_tn]
        tnx = tn
        b_tnx = b_tn
        st = sb("st", [128, 48], F32)
        b_st = [Buf() for _ in range(8)]
        Et = sb("Et", [128, 16], F32)
        b_E = Buf()
        b_E2 = Buf()
        b_qk2 = Buf()

        banks = [es.enter_context(nc.psum_tensor("bank%d" % i, [128, 512], F32)) for i in range(8)]
        b_bank = [Buf() for _ in range(8)]
        bank_i = [0]

        cx0 = dict(k=qk[:, 8:16, :], bk=b_qk, v=v_bf, bv=b_v, rv=rv_bf, brv=b_rv, rkd=rkd, brkd=b_rkd, E=0, bE=b_E,
                   kt=hk, bkt=[b_hk])
        cx1 = dict(k=qk[:, 0:8, :], bk=b_qk2, v=sg_hg, bv=b_sg[0], rv=sg_rt, brv=b_sg[1], rkd=sg_xa, brkd=b_sg[2], E=8, bE=b_E2,
                   kt=AT, bkt=b_AT)
        cx0p = dict(cx0, kt=AT, bkt=b_AT)

        class BankPool:
            def __init__(self, idxs):
                self.idxs = list(idxs)
                self.i = 0

            def next(self):
                k = self.idxs[self.i % len(self.idxs)]
                self.i += 1
                return banks[k], b_bank[k]

        pool_all = BankPool(range(8))

        def nb(pool=None):
            return (pool or pool_all).next()

        op("sp", lambda e: e.dma_start(out=ng_sb[:], in_=ng_d), writes=[b_small])
        op("sp", lambda e: e.dma_start(out=mg_sb[:], in_=mg_d), writes=[b_small])
        op("sp", lambda e: e.dma_start(out=gout_sb[:], in_=gout_d), writes=[b_small])
        op("sp", lambda e: e.dma_start(out=lbl_sb[:], in_=lbl_d), writes=[b_small])
        op("sp", lambda e: e.dma_start(out=posb_sb[:], in_=posb_d), writes=[b_small])
        op("sp", lambda e: e.dma_start(out=gfin_sb[:], in_=gfin_d), writes=[b_gfin])

        op("pool", lambda e: e.memset(onesf[:], 1.0), writes=[b_const])
        op("pool", lambda e: e.affine_select(out=ident[:], in_=onesf[:], pattern=[[1, 128]], compare_op=ALU.is_equal,
                                             fill=0.0, base=0, channel_multiplier=-1), reads=[b_const], writes=[b_const])
        op("pool", lambda e: e.affine_select(out=Mc[:], in_=onesf[:], pattern=[[1, 128]], compare_op=ALU.is_ge,
                                             fill=0.0, base=0, channel_multiplier=-1), reads=[b_const], writes=[b_const])
        dij = tn
        dii = hk[:].rearrange("p a b -> p (a b)").bitcast(I32)[:, 0:128]
        b_dii = b_hk
        op("pool", lambda e: e.iota(dii[:], pattern=[[1, 128]], base=0, channel_multiplier=-1), writes=[b_dii])
        op("pool", lambda e: e.tensor_copy(out=dij[:, 0:128], in_=dii[:]), reads=[b_dii], writes=[b_tn])
        op("dve", lambda e: e.scalar_tensor_tensor(out=dij[:, 128:256], in0=dij[:, 0:128], scalar=-1.0, in1=dij[:, 0:128],
                                                   op0=ALU.mult, op1=ALU.max), reads=[b_tn], writes=[b_tn])
        for h in range(4):
            op("act", lambda e, h=h: e.activation(out=Mrt[:, h, :], in_=dij[:, 128:256], func=AF.Exp, scale=LG[h]),
               reads=[b_tn], writes=[b_const])
        op("pool", lambda e: e.memset(Mrt[64:128, :, 0:64], 0.0), reads=[b_const], writes=[b_const])
        op("pool", lambda e: e.iota(dii[:], pattern=[[1, 128]], base=1, channel_multiplier=0), reads=[b_dii], writes=[b_dii])
        op("pool", lambda e: e.tensor_copy(out=dij[:, 256:384], in_=dii[:]), reads=[b_dii], writes=[b_tn])
        for h in range(4):
            hb = (h % 2) * 64
            op("act", lambda e, h=h, hb=hb: e.activation(out=qdec[hb:hb + 64, h // 2, :], in_=dij[hb:hb + 64, 256:384],
                                                         func=AF.Exp, scale=LG[h]), reads=[b_tn], writes=[b_const])
        for ti, T in enumerate((128, 64)):
            op("pool", lambda e, T=T: e.iota(dii[:, 0:1], pattern=[[0, 1]], base=T - 1, channel_multiplier=-1),
               reads=[b_dii], writes=[b_dii])
            op("pool", lambda e: e.tensor_copy(out=dij[:, 384:385], in_=dii[:, 0:1]), reads=[b_dii], writes=[b_tn])
            for h in range(4):
                op("act", lambda e, h=h, ti=ti: e.activation(out=kdec[:, ti * 4 + h:ti * 4 + h + 1], in_=dij[:, 384:385],
                                                             func=AF.Exp, scale=LG[h]), reads=[b_tn], writes=[b_const])
        op("dve", lambda e: e.tensor_tensor(out=lbc[:, 24:32], in0=lbl_sb[:, 8:16], in1=lbl_sb[:, 0:8], op=ALU.subtract),
           reads=[b_small], writes=[b_lbc])
        op("act", lambda e: e.activation(out=lbc[:, 24:32], in_=lbc[:, 24:32], func=AF.Exp), reads=[b_lbc], writes=[b_lbc])
        op("dve", lambda e: e.tensor_scalar_add(out=lbc[:, 24:32], in0=lbc[:, 24:32], scalar1=1.0), reads=[b_lbc], writes=[b_lbc])
        op("dve", lambda e: e.reciprocal(out=lbc[:, 24:32], in_=lbc[:, 24:32]), reads=[b_lbc], writes=[b_lbc])
        op("dve", lambda e: e.tensor_scalar(out=lbc[:, 0:8], in0=lbc[:, 24:32], scalar1=0.5, scalar2=0.5, op0=ALU.mult, op1=ALU.add),
           reads=[b_lbc], writes=[b_lbc])
        op("dve", lambda e: e.tensor_scalar(out=lbc[:, 8:16], in0=lbc[:, 24:32], scalar1=-0.5, scalar2=0.5, op0=ALU.mult, op1=ALU.add),
           reads=[b_lbc], writes=[b_lbc])
        op("dve", lambda e: e.tensor_scalar(out=lbc[:, 16:24], in0=lbc[:, 24:32], scalar1=0.5, scalar2=-0.5, op0=ALU.mult, op1=ALU.add),
           reads=[b_lbc], writes=[b_lbc])

        stg_i = [0]

        cast_engs = ["pool"]

        def wjob(dram_ap, dst_ap, rows_scale_ap, colscale, reads, writes):
            i = stg_i[0] % len(stage)
            eng = cast_engs[stg_i[0] % len(cast_engs)]
            stg_i[0] += 1
            n = dst_ap.shape[-1]
            src = stage[i][:, 0:n]
            bst = b_stage[i]
            op("sp", lambda e: e.dma_start(out=src, in_=dram_ap), writes=[bst])
            if eng == "act" and colscale != 1.0:
                eng = "dve"
            if rows_scale_ap is None:
                if eng == "act":
                    op("act", lambda e: e.activation(out=dst_ap, in_=src, func=AF.Copy), reads=[bst] + reads, writes=writes)
                else:
                    op(eng, lambda e: e.tensor_copy(out=dst_ap, in_=src), reads=[bst] + reads, writes=writes)
            elif eng == "act":
                op("act", lambda e: e.activation(out=dst_ap, in_=src, func=AF.Copy, scale=rows_scale_ap),
                   reads=[bst, b_small] + reads, writes=writes)
            else:
                op(eng, lambda e: e.tensor_scalar(out=dst_ap, in0=src, scalar1=rows_scale_ap, scalar2=colscale,
                                                  op0=ALU.mult, op1=ALU.mult), reads=[bst, b_small] + reads, writes=writes)

        def win_job(kc, c0, n, colscale=1.0):
            wjob(win_d[kc * 128:(kc + 1) * 128, c0:c0 + n], win_sb[:, kc, c0:c0 + n], ng_sb[:, kc:kc + 1], colscale,
                 [], [b_win[kc][c0 // 512]])

        pre_cols = [(C_F, 512), (C_F + 512, 512), (C_I, 512), (C_I + 512, 512), (C_RK, 256), (C_RV, 512)]
        rest_cols = [(C_Q, 512), (C_Q + 512, 512), (C_XQ, 512), (C_G, 512), (C_G + 512, 512), (C_RQ, 256),
                     (C_RG, 512), (C_XG, 512)]
        cast_engs[:] = ["dve", "act", "pool", "dve", "act"]
        stage_all = list(stage)
        bstage_all = list(b_stage)
        stage.extend([xt[0][:, 0:512], xt[0][:, 512:1024], xt[1][:, 0:512], xt[1][:, 512:1024]])
        b_sx = [Buf() for _ in range(4)]
        b_stage.extend(b_sx)
        NSTG = 6
        for c0, n in pre_cols:
            for kc in range(8):
                win_job(kc, c0, n, 0.125 if c0 == C_RK else 1.0)
        cast_engs[:] = ["pool", "act"]
        del stage[2:]
        del b_stage[2:]
        NSTG = 2

        om = rtmp[:, 0, :]
        op("pool", lambda e: e.iota(rti[:], pattern=[[1, 32]], base=0, channel_multiplier=0), writes=[b_rtmp])
        op("pool", lambda e: e.tensor_copy(out=rtmp[:, 0, :], in_=rti[:]), reads=[b_rtmp], writes=[b_rtmp])
        op("act", lambda e: e.activation(out=rtmp[:, 0, :], in_=rtmp[:, 0, :], func=AF.Exp, scale=-math.log(10000.0) / 32.0),
           reads=[b_rtmp], writes=[b_rtmp])

        a6 = tn[:, 0:192]
        q6 = tn[:, 192:384]
        m6 = BtAll[:, 0:192]
        i6 = hk[:].rearrange("p a b -> p (a b)").bitcast(I32)[:, 0:192]
        b_r6 = [b_tn, b_Bt[0], b_hk]
        specs = [(posb_sb[:, 0:1], 0.0), (posb_sb[:, 0:1], math.pi / 2), (posb_sb[:, 1:2], 0.0), (posb_sb[:, 1:2], math.pi / 2),
                 (128.0, 0.0), (128.0, math.pi / 2)]
        for k, (pm, sh) in enumerate(specs):
            op("dve", lambda e, k=k, pm=pm, sh=sh: e.tensor_scalar(out=a6[:, k * 32:(k + 1) * 32], in0=om, scalar1=pm, scalar2=sh,
                                                                    op0=ALU.mult, op1=ALU.add),
               reads=[b_rtmp, b_small], writes=b_r6)
        op("dve", lambda e: e.tensor_scalar_mul(out=q6, in0=a6, scalar1=1.0 / TWO_PI), reads=b_r6, writes=b_r6)
        op("dve", lambda e: e.tensor_copy(out=i6, in_=q6), reads=b_r6, writes=b_r6)
        op("dve", lambda e: e.tensor_copy(out=q6, in_=i6), reads=b_r6, writes=b_r6)
        op("dve", lambda e: e.scalar_tensor_tensor(out=a6, in0=q6, scalar=-CW1, in1=a6, op0=ALU.mult, op1=ALU.add), reads=b_r6, writes=b_r6)
        op("dve", lambda e: e.scalar_tensor_tensor(out=a6, in0=q6, scalar=-CW2, in1=a6, op0=ALU.mult, op1=ALU.add), reads=b_r6, writes=b_r6)
        for _ in range(2):
            op("dve", lambda e: e.tensor_single_scalar(out=m6, in_=a6, scalar=math.pi, op=ALU.is_gt), reads=b_r6, writes=b_r6)
            op("dve", lambda e: e.scalar_tensor_tensor(out=a6, in0=m6, scalar=-TWO_PI, in1=a6, op0=ALU.mult, op1=ALU.add), reads=b_r6, writes=b_r6)
            op("dve", lambda e: e.tensor_single_scalar(out=m6, in_=a6, scalar=-math.pi, op=ALU.is_lt), reads=b_r6, writes=b_r6)
            op("dve", lambda e: e.scalar_tensor_tensor(out=a6, in0=m6, scalar=TWO_PI, in1=a6, op0=ALU.mult, op1=ALU.add), reads=b_r6, writes=b_r6)
        op("act", lambda e: e.activation(out=rtab[:, 0:2, :], in_=a6[:, 0:64].rearrange("p (k i) -> p k i", i=32), func=AF.Sin),
           reads=b_r6, writes=[b_rtab[0]])
        op("act", lambda e: e.activation(out=rtab[:, 4:8, :], in_=a6[:, 64:192].rearrange("p (k i) -> p k i", i=32), func=AF.Sin),
           reads=b_r6, writes=[b_rtab[2], b_rtab[3]])

        bg_jobs = []
        for w, wd in enumerate((wmk_d, wmv_d)):
            for kc in range(8):
                bg_jobs.append(lambda w=w, wd=wd, kc=kc: wjob(wd[kc * 128:(kc + 1) * 128, :], wmem4[:, w, kc, :],
                                                            mg_sb[:, kc:kc + 1], 1.0, [], [b_wmem[w]]))
        for kc in range(8):
            for c0, n in rest_cols:
                bg_jobs.append(lambda kc=kc, c0=c0, n=n: win_job(kc, c0, n))
        wout_jobs = []
        for kc in range(16):
            for nbk in range(2):
                if kc < 12:
                    wout_jobs.append(lambda kc=kc, nbk=nbk: wjob(wout_d[kc * 128:(kc + 1) * 128, nbk * 512:(nbk + 1) * 512],
                                                                 wout3[:, kc, nbk * 512:(nbk + 1) * 512],
                                                                 gout_sb[:, kc:kc + 1], 1.0, [], [b_wout[kc], b_Bx, b_hk2] + b_wmem))
                else:
                    wout_jobs.append(lambda kc=kc, nbk=nbk: wjob(wout_d[kc * 128:(kc + 1) * 128, nbk * 512:(nbk + 1) * 512],
                                                                 wout3[:, kc, nbk * 512:(nbk + 1) * 512],
                                                                 None, 1.0, [], [b_wout[kc], b_Bx, b_hk2] + b_wmem))

        def wr(kc, c0):
            return b_win[kc][c0 // 512]

        def rms_rstd(src_ap, src_buf, T, stcol, scale):
            sc = st[0:T, stcol:stcol + 1]
            op("act", lambda e: e.activation(out=junk[0:T, 0:src_ap.shape[-1]], in_=src_ap, func=AF.Square, accum_out=sc),
               reads=[src_buf], writes=[b_junk, b_st[0]])
            op("act", lambda e: e.activation(out=sc, in_=sc, func=AF.Ln, bias=EPS, scale=scale), reads=[b_st[0]], writes=[b_st[0]])
            op("act", lambda e: e.activation(out=sc, in_=sc, func=AF.Exp, scale=-0.5), reads=[b_st[0]], writes=[b_st[0]])

        def norm_transpose(src_ap, src_buf, T):
            rms_rstd(src_ap, src_buf, T, 0, 1.0 / D)
            op("dve", lambda e: e.tensor_scalar_mul(out=scrA[0:T, 0:D], in0=src_ap, scalar1=st[0:T, 0:1]),
               reads=[src_buf, b_st[0]], writes=[b_scrA])
            bk, bb = nb()
            bkb = bk[:].bitcast(BF16)
            for kc in range(8):
                op("pe", lambda e, kc=kc: e.transpose(out=bkb[:, kc * T:(kc + 1) * T], in_=scrA[0:T, kc * 128:(kc + 1) * 128],
                                                      identity=ident[0:T, 0:T]), reads=[b_scrA, b_const], writes=[bb])
            op("act", lambda e: e.activation(out=hk[:, :, 0:T], in_=bkb[:, 0:8 * T].rearrange("p (k t) -> p k t", t=T), func=AF.Copy),
               reads=[bb], writes=[b_hk])

        def fm_group(c0, T, pool=None, hT=None, bhT=None):
            bk, bb = nb(pool)
            hT = hk if hT is None else hT
            bhT = b_hk if bhT is None else bhT
            for hh in range(4):
                for kc in range(8):
                    op("pe", lambda e, hh=hh, kc=kc: e.matmul(bk[:, hh * T:(hh + 1) * T],
                                                              win_sb[:, kc, c0 + hh * 128:c0 + (hh + 1) * 128], hT[:, kc, 0:T],
                                                              start=(kc == 0), stop=(kc == 7)),
                       reads=[bhT, wr(kc, c0)], writes=[bb])
            return bk, bb

        def tm_group(c0, n, T, col_off=0, pool=None, hT=None, bhT=None):
            bk, bb = nb(pool)
            hT = hk if hT is None else hT
            bhT = b_hk if bhT is None else bhT
            for kc in range(8):
                op("pe", lambda e, kc=kc: e.matmul(bk[0:T, col_off:col_off + n], hT[:, kc, 0:T], win_sb[:, kc, c0:c0 + n],
                                                   start=(kc == 0), stop=(kc == 7)),
                   reads=[bhT, wr(kc, c0)], writes=[bb])
            return bk, bb

        def f_chain(hf, T, need_ep, cx):
            bk, bb = fm_group(C_F + hf * 512, T)
            n = 4 * T
            th, kk, gc = Bt[0], Bt[1], Bt[2]
            op("act", lambda e: e.activation(out=th[:, 0:n], in_=bk[:, 0:n], func=AF.Tanh, scale=0.5), reads=[bb], writes=[b_Bt[0]])
            for hh in range(4):
                h = hf * 4 + hh
                op("dve", lambda e, hh=hh, h=h: e.tensor_scalar(out=kk[:, hh * T:(hh + 1) * T], in0=th[:, hh * T:(hh + 1) * T],
                                                                scalar1=lbc[:, 16 + h:17 + h], scalar2=lbc[:, 8 + h:9 + h],
                                                                op0=ALU.mult, op1=ALU.add),
                   reads=[b_Bt[0], b_lbc], writes=[b_Bt[1]])
            for hh in range(4):
                h = hf * 4 + hh
                op("act", lambda e, hh=hh, h=h: e.activation(out=th[:, hh * T:(hh + 1) * T], in_=th[:, hh * T:(hh + 1) * T], func=AF.Ln,
                                                             bias=lbc[:, h:h + 1], scale=lbc[:, 8 + h:9 + h]),
                   reads=[b_Bt[0], b_lbc], writes=[b_Bt[0]])
            for hh in range(4):
                op("dve", lambda e, hh=hh: e.tensor_tensor_scan(out=gc[:, hh * T:(hh + 1) * T], data0=onesf[:, 0:T],
                                                                data1=th[:, hh * T:(hh + 1) * T], initial=0.0,
                                                                op0=ALU.mult, op1=ALU.add),
                   reads=[b_Bt[0], b_const], writes=[b_Bt[2]])
            gc3 = gc[:, 0:n].rearrange("p (h t) -> p h t", t=T)
            e0 = cx["E"] + hf * 4
            op("act", lambda e: e.activation(out=Et[:, e0:e0 + 4], in_=gc3[:, :, T - 1], func=AF.Exp),
               reads=[b_Bt[2]], writes=[cx["bE"]])
            if need_ep:
                op("act", lambda e: e.activation(out=th[:, 0:n], in_=gc[:, 0:n], func=AF.Exp), reads=[b_Bt[2]], writes=[b_Bt[0]])
            op("act", lambda e: e.activation(out=gc[:, 0:n], in_=gc[:, 0:n], func=AF.Exp, scale=-1.0), reads=[b_Bt[2]], writes=[b_Bt[2]])
            op("pool", lambda e: e.tensor_tensor(out=cx["k"][:, hf * 4:hf * 4 + 4, 0:T], in0=kk[:, 0:n].rearrange("p (h t) -> p h t", t=T),
                                                 in1=gc3, op=ALU.mult), reads=[b_Bt[1], b_Bt[2]], writes=[cx["bk"]])

        def q_chain(hf, T):
            bk, bb = fm_group(C_Q + hf * 512, T)
            n = 4 * T
            qs = Bt[1]
            op("act", lambda e: e.activation(out=qs[:, 0:n], in_=bk[:, 0:n], func=AF.Silu), reads=[bb], writes=[b_Bt[1]])
            op("pool", lambda e: e.tensor_tensor(out=qk[:, hf * 4:hf * 4 + 4, 0:T], in0=qs[:, 0:n].rearrange("p (h t) -> p h t", t=T),
                                                 in1=Bt[0][:, 0:n].rearrange("p (h t) -> p h t", t=T), op=ALU.mult),
               reads=[b_Bt[1], b_Bt[0]], writes=[b_qk, b_qk2])

        def rope(bk, bb, T, g0, g1, tabi, tabbuf):
            ng_ = g1 - g0
            X = bk[0:T, g0 * 64:g1 * 64].rearrange("p (g two i) -> p g two i", two=2, i=32)
            O = rqk[0:T, g0 * 64:g1 * 64].rearrange("p (g two i) -> p g two i", two=2, i=32)
            sn = rtab[0:T, tabi, :].unsqueeze(1).broadcast_to([T, ng_, 32])
            cs = rtab[0:T, tabi + 1, :].unsqueeze(1).broadcast_to([T, ng_, 32])
            t1 = rop[0:T, 0, 0:ng_ * 32].rearrange("p (g i) -> p g i", i=32)
            t2 = rop[0:T, 1, 0:ng_ * 32].rearrange("p (g i) -> p g i", i=32)
            rd = [bb, tabbuf]
            op("dve", lambda e: e.tensor_tensor(out=t1, in0=X[:, :, 0, :], in1=cs, op=ALU.mult), reads=rd, writes=[b_rop])
            op("dve", lambda e: e.tensor_tensor(out=t2, in0=X[:, :, 1, :], in1=sn, op=ALU.mult), reads=rd, writes=[b_rop])
            op("dve", lambda e: e.tensor_tensor(out=O[:, :, 0, :], in0=t1, in1=t2, op=ALU.subtract), reads=[b_rop], writes=[b_rqk])
            op("dve", lambda e: e.tensor_tensor(out=t1, in0=X[:, :, 0, :], in1=sn, op=ALU.mult), reads=rd + [b_rqk], writes=[b_rop])
            op("dve", lambda e: e.tensor_tensor(out=t2, in0=X[:, :, 1, :], in1=cs, op=ALU.mult), reads=rd, writes=[b_rop])
            op("dve", lambda e: e.tensor_tensor(out=O[:, :, 1, :], in0=t1, in1=t2, op=ALU.add), reads=[b_rop], writes=[b_rqk])

        def rope_advance(cur):
            s0, c0_ = rtab[:, 2 * cur, :], rtab[:, 2 * cur + 1, :]
            s1, c1_ = rtab[:, 2 * (1 - cur), :], rtab[:, 2 * (1 - cur) + 1, :]
            S, C = rtab[:, 6, :], rtab[:, 7, :]
            t = [rtmp[:, k, :] for k in range(1, 5)]
            rd = [b_rtab[cur], b_rtab[3]]
            op("pool", lambda e: e.tensor_tensor(out=t[0], in0=c0_, in1=C, op=ALU.mult), reads=rd, writes=[b_rtmp])
            op("pool", lambda e: e.tensor_tensor(out=t[1], in0=s0, in1=S, op=ALU.mult), reads=rd, writes=[b_rtmp])
            op("pool", lambda e: e.tensor_tensor(out=t[2], in0=s0, in1=C, op=ALU.mult), reads=rd, writes=[b_rtmp])
            op("pool", lambda e: e.tensor_tensor(out=t[3], in0=c0_, in1=S, op=ALU.mult), reads=rd, writes=[b_rtmp])
            op("pool", lambda e: e.tensor_tensor(out=c1_, in0=t[0], in1=t[1], op=ALU.subtract), reads=[b_rtmp], writes=[b_rtab[1 - cur]])
            op("pool", lambda e: e.tensor_tensor(out=s1, in0=t[2], in1=t[3], op=ALU.add), reads=[b_rtmp], writes=[b_rtab[1 - cur]])

        def state_hg_T(T, cx, pool=None):
            bk, bb = nb(pool)
            bkb = bk[:].bitcast(BF16)
            for h in range(8):
                op("pe", lambda e, h=h: e.transpose(out=bkb[0:T, h * 128:(h + 1) * 128], in_=cx["k"][:, h, 0:T], identity=ident[:, :]),
                   reads=[cx["bk"], b_const], writes=[bb])
            op("dve", lambda e: e.tensor_copy(out=cx["kt"][0:T, :, :], in_=bkb[0:T, :].rearrange("p (h d) -> p h d", d=128)),
               reads=[bb], writes=cx["bkt"])

        def state_hg_U_g(T, want_bf, cx, gen=True, pool=None):
            for hf in range(2):
                bk2, bb2 = nb(pool)
                for hh in range(4):
                    h = hf * 4 + hh
                    op("pe", lambda e, hh=hh, h=h, bk2=bk2: e.matmul(bk2[:, hh * 128:(hh + 1) * 128], cx["kt"][0:T, h, :],
                                                                     cx["v"][0:T, h * 128:(h + 1) * 128], start=True, stop=True),
                       reads=cx["bkt"] + [cx["bv"]], writes=[bb2])
                if gen:
                    yield
                S2 = S_sb[:, hf * 4:hf * 4 + 4, :]
                op("dve", lambda e, S2=S2, bk2=bk2: e.tensor_tensor(out=S2, in0=S2, in1=bk2[:, :].rearrange("p (h v) -> p h v", v=128),
                                                                     op=ALU.add),
                   reads=[bb2] + b_S[hf * 4:hf * 4 + 4], writes=b_S[hf * 4:hf * 4 + 4])
                ec = cx["E"] + hf * 4
                op("dve", lambda e, S2=S2, ec=ec: e.tensor_tensor(out=S2, in0=S2, in1=Et[:, ec:ec + 4].unsqueeze(2).broadcast_to([128, 4, 128]),
                                                                   op=ALU.mult),
                   reads=[cx["bE"]] + b_S[hf * 4:hf * 4 + 4], writes=b_S[hf * 4:hf * 4 + 4])
                if gen:
                    yield
            if want_bf:
                op("pool", lambda e: e.tensor_copy(out=Sbf[:], in_=S_sb[:]), reads=b_S, writes=[b_Sbf])

        def state_hg_U(T, want_bf, cx):
            for _ in state_hg_U_g(T, want_bf, cx):
                pass

        def state_update_hg(T, want_bf, cx):
            state_hg_T(T, cx)
            state_hg_U(T, want_bf, cx)

        def make_rkd(T, cx):
            ti = 0 if T == 128 else 1
            op("dve", lambda e: e.tensor_tensor(out=cx["rkd"][0:T, 0:256].rearrange("p (h d) -> p h d", d=64),
                                                in0=rqk[0:T, 256:512].rearrange("p (h d) -> p h d", d=64),
                                                in1=kdec[0:T, ti * 4:ti * 4 + 4].unsqueeze(2).broadcast_to([T, 4, 64]), op=ALU.mult),
               reads=[b_rqk, b_const], writes=[cx["brkd"]])

        def state_update_rt(T, want_bf, cx, pool=None):
            bk, bb = nb(pool)
            for pr in range(2):
                op("pe", lambda e, pr=pr: e.matmul(bk[:, pr * 256:(pr + 1) * 256], cx["rkd"][0:T, pr * 128:(pr + 1) * 128],
                                                   cx["rv"][0:T, pr * 256:(pr + 1) * 256], start=True, stop=True),
                   reads=[cx["brkd"], cx["brv"]], writes=[bb])
            for h in range(4):
                hb = (h % 2) * 64
                pr = h // 2
                g = math.exp(LG[h] * T)
                op("dve", lambda e, hb=hb, pr=pr, g=g, h=h: e.scalar_tensor_tensor(
                    out=R_sb[hb:hb + 64, pr, :], in0=R_sb[hb:hb + 64, pr, :], scalar=g,
                    in1=bk[hb:hb + 64, pr * 256 + (h % 2) * 128:pr * 256 + (h % 2) * 128 + 128], op0=ALU.mult, op1=ALU.add),
                   reads=[bb, b_R], writes=[b_R])
            if want_bf:
                op("pool", lambda e: e.tensor_copy(out=Rbf[:], in_=R_sb[:]), reads=[b_R], writes=[b_Rbf])

        x_extra = {}

        def load_x(x_dram_rows, T, slot):
            op("sp", lambda e: e.dma_start(out=xt[slot][0:T, :], in_=x_dram_rows), writes=[b_xt[slot]] + x_extra.pop(slot, []))

        b_mix = [Buf(), Buf(), Buf()]

        def chain(*gens):
            for g in gens:
                for _ in g:
                    yield

        def fq_half_gen(T, cx, hf, B3, bB3, pool):
            n = 4 * T
            Bu, Bg, Bw = B3
            bu, bg, bw = bB3
            bkf, bbf = fm_group(C_F + hf * 512, T, pool)
            yield
            bkq, bbq = fm_group(C_Q + hf * 512, T, pool)
            yield
            op("act", lambda e: e.activation(out=Bu[:, 0:n], in_=bkf[:, 0:n], func=AF.Exp, scale=-1.0), reads=[bbf], writes=bu)
            op("act", lambda e: e.activation(out=Bw[:, 0:n], in_=bkq[:, 0:n], func=AF.Exp, scale=-1.0), reads=[bbq], writes=bw)
            yield
            for hh in range(4):
                h = hf * 4 + hh
                op("act", lambda e, hh=hh, h=h: e.activation(out=Bg[:, hh * T:(hh + 1) * T], in_=Bu[:, hh * T:(hh + 1) * T], func=AF.Ln,
                                                             bias=1.0, scale=lbc[:, 24 + h:25 + h]),
                   reads=bu + [b_lbc], writes=bg)
            yield
            op("act", lambda e: e.activation(out=Bu[:, 0:n], in_=Bu[:, 0:n], func=AF.Ln, bias=1.0, scale=1.0), reads=bu, writes=bu)
            op("act", lambda e: e.activation(out=Bw[:, 0:n], in_=Bw[:, 0:n], func=AF.Ln, bias=1.0, scale=1.0), reads=bw, writes=bw)
            yield
            op("dve", lambda e: e.tensor_tensor(out=Bg[:, 0:n], in0=Bg[:, 0:n], in1=Bu[:, 0:n], op=ALU.subtract),
               reads=bg + bu, writes=bg)
            yield
            for hh in range(4):
                op("dve", lambda e, hh=hh: e.tensor_tensor_scan(out=Bu[:, hh * T:(hh + 1) * T], data0=onesf[:, 0:T],
                                                                data1=Bg[:, hh * T:(hh + 1) * T], initial=0.0,
                                                                op0=ALU.mult, op1=ALU.add),
                   reads=bg + [b_const], writes=bu)
                yield
            gc3 = Bu[:, 0:n].rearrange("p (h t) -> p h t", t=T)
            e0 = cx["E"] + hf * 4
            op("act", lambda e: e.activation(out=Et[:, e0:e0 + 4], in_=gc3[:, :, T - 1], func=AF.Exp),
               reads=bu, writes=[cx["bE"]])
            op("act", lambda e: e.activation(out=Bg[:, 0:n], in_=Bu[:, 0:n], func=AF.Exp, scale=-1.0), reads=bu, writes=bg)
            op("dve", lambda e: e.tensor_tensor(out=Bw[:, 0:n], in0=Bu[:, 0:n], in1=Bw[:, 0:n], op=ALU.subtract),
               reads=bu + bw, writes=bw)
            yield
            op("act", lambda e: e.activation(out=Bw[:, 0:n], in_=Bw[:, 0:n], func=AF.Exp), reads=bw, writes=bw)
            em3 = Bg[:, 0:n].rearrange("p (h t) -> p h t", t=T)
            kd = cx["k"][:, hf * 4:hf * 4 + 4, :]
            op("dve", lambda e: e.tensor_tensor(out=kd[:, :, 1:T], in0=em3[:, :, 1:T], in1=em3[:, :, 0:T - 1], op=ALU.subtract),
               reads=bg, writes=[cx["bk"]])
            op("dve", lambda e: e.tensor_scalar_add(out=kd[:, :, 0], in0=em3[:, :, 0], scalar1=-1.0),
               reads=bg, writes=[cx["bk"]])
            yield
            op("dve", lambda e: e.tensor_tensor(out=qk[:, hf * 4:hf * 4 + 4, 0:T],
                                                in0=bkq[:, 0:n].rearrange("p (h t) -> p h t", t=T),
                                                in1=Bw[:, 0:n].rearrange("p (h t) -> p h t", t=T), op=ALU.mult),
               reads=[bbq] + bw, writes=[b_qk, b_qk2])
            yield

        def fq_gen(T, cx):
            pool = BankPool([0, 1])
            B3 = (Bt[0], Bt[1], Bt[2])
            bB3 = ([b_Bt[0]], [b_Bt[1]], [b_Bt[2]])
            return chain(fq_half_gen(T, cx, 0, B3, bB3, pool), fq_half_gen(T, cx, 1, B3, bB3, pool))

        def fq_par_gens(T, cx):
            B3a = (Bt[0], Bt[1], Bt[2])
            bB3a = ([b_Bt[0]], [b_Bt[1]], [b_Bt[2]])
            B3b = (AT[:].rearrange("p a b -> p (a b)").bitcast(F32), xP[:].rearrange("p a b -> p (a b)").bitcast(F32), tn[:, :])
            bB3b = (list(b_AT), list(b_xP), [b_tn])
            return [fq_half_gen(T, cx, 0, B3a, bB3a, BankPool([0, 1])), fq_half_gen(T, cx, 1, B3b, bB3b, BankPool([2, 3]))]

        def tok_gen(T, cx, tabi, tabbuf, banks=(2, 3, 4, 5, 6)):
            pool = BankPool(banks)
            for half in range(2):
                bk, bb = tm_group(C_I + half * 512, 512, T, 0, pool)
                op("act", lambda e, bk=bk, half=half: e.activation(out=v_bf[0:T, half * 512:(half + 1) * 512], in_=bk[0:T, :], func=AF.Copy),
                   reads=[bb], writes=[b_v])
                yield
            bk, bb = tm_group(C_RQ, 512, T, 0, pool)
            yield
            rope(bk, bb, T, 0, 8, tabi, tabbuf)
            yield
            make_rkd(T, cx)
            bk, bb = tm_group(C_RV, 512, T, 0, pool)
            yield
            op("dve", lambda e, bk=bk: e.tensor_copy(out=rv_bf[0:T, :], in_=bk[0:T, :]), reads=[bb], writes=[b_rv])
            yield
            bk, bb = fm_group(C_XQ, T, pool)
            op("act", lambda e, bk=bk: e.activation(out=xqT[:, :, 0:T], in_=bk[:, 0:4 * T].rearrange("p (h t) -> p h t", t=T),
                                                    func=AF.Copy, scale=128.0 ** -0.5), reads=[bb], writes=[b_xqT])
            yield
            gb = []
            for half in range(2):
                gb.append(tm_group(C_G + half * 512, 512, T, 0, pool))
                yield
            gb.append(tm_group(C_RG, 512, T, 0, pool))
            yield
            gb.append(tm_group(C_XG, 512, T, 0, pool))
            yield
            for half in range(2):
                bk, bb = gb[half]
                op("act", lambda e, bk=bk, half=half: e.activation(out=sg_hg[0:T, half * 512:(half + 1) * 512], in_=bk[0:T, :], func=AF.Silu),
                   reads=[bb], writes=[b_sg[0]])
            bk, bb = gb[2]
            op("act", lambda e, bk=bk: e.activation(out=sg_rt[0:T, :], in_=bk[0:T, :], func=AF.Silu), reads=[bb], writes=[b_sg[1]])
            bk, bb = gb[3]
            op("act", lambda e, bk=bk: e.activation(out=sg_xa[0:T, :], in_=bk[0:T, :], func=AF.Silu), reads=[bb], writes=[b_sg[2]])
            yield

        def hgrn_gen(T, want_bf, cx):
            pool = BankPool([0, 1, 2, 3])
            sb_ = []
            for hf in range(2):
                bk, bb = nb(pool)
                for hh in range(4):
                    h = hf * 4 + hh
                    op("pe", lambda e, bk=bk, hh=hh, h=h: e.matmul(bk[0:T, hh * T:(hh + 1) * T], qk[:, 8 + h, 0:T], qk[:, h, 0:T],
                                                                   start=True, stop=True), reads=[b_qk], writes=[bb])
                sb_.append((bk, bb))
                yield
            for hf in range(2):
                bk, bb = sb_[hf]
                op("dve", lambda e, bk=bk, hf=hf: e.tensor_tensor(out=AT[0:T, hf * 4:hf * 4 + 4, 0:T],
                                                                 in0=bk[0:T, 0:4 * T].rearrange("p (h t) -> p h t", t=T),
                                                                 in1=Mc[0:T, 0:T].unsqueeze(1).broadcast_to([T, 4, T]), op=ALU.mult),
                   reads=[bb, b_const], writes=[b_AT[hf]])
                yield
            obanks = []
            for hf in range(2):
                bk, bb = nb(pool)
                for hh in range(4):
                    h = hf * 4 + hh
                    op("pe", lambda e, bk=bk, hh=hh, h=h: e.matmul(bk[0:T, hh * 128:(hh + 1) * 128], AT[0:T, h, 0:T],
                                                                   v_bf[0:T, h * 128:(h + 1) * 128], start=True, stop=False),
                       reads=[b_AT[hf], b_v], writes=[bb])
                    op("pe", lambda e, bk=bk, hh=hh, h=h: e.matmul(bk[0:T, hh * 128:(hh + 1) * 128], qk[:, h, 0:T], Sbf[:, h, :],
                                                                   start=False, stop=True), reads=[b_qk, b_Sbf], writes=[bb])
                obanks.append((bk, bb))
                yield
            state_hg_T(T, cx, pool)
            yield
            for hf in range(2):
                bk, bb = obanks[hf]
                for hh in range(4):
                    h = hf * 4 + hh
                    op("act", lambda e, bk=bk, hh=hh, h=h: e.activation(out=junk[0:T, h * 128:(h + 1) * 128], in_=bk[0:T, hh * 128:(hh + 1) * 128],
                                                                        func=AF.Square, accum_out=st[0:T, 8 + h:9 + h]),
                       reads=[bb], writes=[b_jk[0], b_st[1]])
                yield
            op("act", lambda e: e.activation(out=st[0:T, 8:16], in_=st[0:T, 8:16], func=AF.Ln, bias=EPS, scale=1.0 / 128), reads=[b_st[1]], writes=[b_st[1]])
            op("act", lambda e: e.activation(out=st[0:T, 8:16], in_=st[0:T, 8:16], func=AF.Exp, scale=-0.5), reads=[b_st[1]], writes=[b_st[1]])
            yield
            for hf in range(2):
                bk, bb = obanks[hf]
                for hh in range(4):
                    h = hf * 4 + hh
                    op("dve", lambda e, bk=bk, hh=hh, h=h: e.scalar_tensor_tensor(
                        out=scrA[0:T, h * 128:(h + 1) * 128], in0=bk[0:T, hh * 128:(hh + 1) * 128], scalar=st[0:T, 8 + h:9 + h],
                        in1=sg_hg[0:T, h * 128:(h + 1) * 128], op0=ALU.mult, op1=ALU.mult),
                       reads=[bb, b_st[1], b_sg[0]], writes=[b_mix[0]])
                yield
            for _ in state_hg_U_g(T, want_bf, cx, True, pool):
                yield

        def ret_gen(T, want_bf, cx):
            pool = BankPool([4, 5])
            bk, bb = nb(pool)
            bkb = bk[:].bitcast(BF16)
            for g in range(4):
                op("pe", lambda e, g=g, bkb=bkb: e.transpose(out=bkb[:, g * T:(g + 1) * T], in_=rqk[0:T, g * 128:(g + 1) * 128],
                                                             identity=ident[0:T, 0:T]), reads=[b_rqk, b_const], writes=[bb])
            yield
            op("act", lambda e, bkb=bkb: e.activation(out=rqkT[:, 2:4, 0:T], in_=bkb[:, 2 * T:4 * T].rearrange("p (g t) -> p g t", t=T), func=AF.Copy),
               reads=[bb], writes=[b_rqkT])
            for h in range(4):
                hb = (h % 2) * 64
                pr = h // 2
                op("act", lambda e, bkb=bkb, h=h, hb=hb, pr=pr: e.activation(out=rqTm[hb:hb + 64, h, 0:T], in_=bkb[hb:hb + 64, pr * T:(pr + 1) * T], func=AF.Copy),
                   reads=[bb], writes=[b_rqkT])
                op("dve", lambda e, bkb=bkb, h=h, hb=hb, pr=pr: e.tensor_tensor(out=rqdTm[hb:hb + 64, h, 0:T], in0=bkb[hb:hb + 64, pr * T:(pr + 1) * T],
                                                                              in1=qdec[hb:hb + 64, pr, 0:T], op=ALU.mult),
                   reads=[bb, b_const], writes=[b_rqkT])
            yield
            bk, bb = nb(pool)
            for h in range(4):
                pr = h // 2
                op("pe", lambda e, bk=bk, h=h, pr=pr: e.matmul(bk[0:T, h * T:(h + 1) * T], rqkT[:, 2 + pr, 0:T],
                                                               rqTm[:, h, 0:T], start=True, stop=True),
                   reads=[b_rqkT], writes=[bb])
            yield
            op("dve", lambda e, bk=bk: e.tensor_tensor(out=PT[0:T, :, 0:T], in0=bk[0:T, 0:4 * T].rearrange("p (h t) -> p h t", t=T),
                                                       in1=Mrt[0:T, :, 0:T], op=ALU.mult), reads=[bb, b_const], writes=[b_PT])
            yield
            bk, bb = nb(pool)
            for h in range(4):
                pr = h // 2
                op("pe", lambda e, bk=bk, h=h: e.matmul(bk[0:T, h * 128:(h + 1) * 128], PT[0:T, h, 0:T], rv_bf[0:T, h * 128:(h + 1) * 128],
                                                        start=True, stop=False), reads=[b_PT, b_rv], writes=[bb])
                op("pe", lambda e, bk=bk, h=h, pr=pr: e.matmul(bk[0:T, h * 128:(h + 1) * 128], rqdTm[:, h, 0:T],
                                                               Rbf[:, pr, :], start=False, stop=True),
                   reads=[b_rqkT, b_Rbf], writes=[bb])
            yield
            for h in range(4):
                op("act", lambda e, bk=bk, h=h: e.activation(out=junk[0:T, 512 + h * 128:512 + (h + 1) * 128], in_=bk[0:T, h * 128:(h + 1) * 128], func=AF.Copy,
                                                             accum_out=st[0:T, 16 + h:17 + h]), reads=[bb], writes=[b_jk[1], b_st[2]])
            yield
            for h in range(4):
                op("act", lambda e, bk=bk, h=h: e.activation(out=junk[0:T, 512 + h * 128:512 + (h + 1) * 128], in_=bk[0:T, h * 128:(h + 1) * 128], func=AF.Square,
                                                             accum_out=st[0:T, 20 + h:21 + h]), reads=[bb], writes=[b_jk[1], b_st[3]])
            yield
            op("dve", lambda e: e.tensor_scalar_mul(out=st[0:T, 16:20], in0=st[0:T, 16:20], scalar1=1.0 / 128), reads=[b_st[2]], writes=[b_st[2]])
            op("dve", lambda e: e.tensor_tensor(out=st[0:T, 24:28], in0=st[0:T, 16:20], in1=st[0:T, 16:20], op=ALU.mult),
               reads=[b_st[2]], writes=[b_st[4]])
            op("dve", lambda e: e.scalar_tensor_tensor(out=st[0:T, 20:24], in0=st[0:T, 20:24], scalar=1.0 / 128, in1=st[0:T, 24:28],
                                                       op0=ALU.mult, op1=ALU.subtract), reads=[b_st[3], b_st[4]], writes=[b_st[3]])
            yield
            op("act", lambda e: e.activation(out=st[0:T, 20:24], in_=st[0:T, 20:24], func=AF.Ln, bias=EPS, scale=1.0), reads=[b_st[3]], writes=[b_st[3]])
            op("act", lambda e: e.activation(out=st[0:T, 20:24], in_=st[0:T, 20:24], func=AF.Exp, scale=-0.5), reads=[b_st[3]], writes=[b_st[3]])
            yield
            for h in range(4):
                op("dve", lambda e, bk=bk, h=h: e.scalar_tensor_tensor(out=tnx[0:T, h * 128:(h + 1) * 128], in0=bk[0:T, h * 128:(h + 1) * 128],
                                                                       scalar=st[0:T, 16 + h:17 + h], in1=sg_rt[0:T, h * 128:(h + 1) * 128],
                                                                       op0=ALU.subtract, op1=ALU.mult),
                   reads=[bb, b_st[2], b_sg[1]], writes=[b_tnx])
                op("pool", lambda e, h=h: e.tensor_scalar(out=scrA[0:T, 1024 + h * 128:1024 + (h + 1) * 128], in0=tnx[0:T, h * 128:(h + 1) * 128],
                                                          scalar1=st[0:T, 20 + h:21 + h], scalar2=1.0, op0=ALU.mult, op1=ALU.mult),
                   reads=[b_tnx, b_st[3]], writes=[b_mix[1]])
                yield
            state_update_rt(T, want_bf, cx, pool)
            yield

        def xa_gen(T):
            pool = BankPool([6, 7])
            xb = []
            for hp in range(2):
                bk, bb = nb(pool)
                for hh in range(2):
                    h = hp * 2 + hh
                    for mc in range(2):
                        op("pe", lambda e, bk=bk, hh=hh, h=h, mc=mc: e.matmul(bk[:, (hh * 2 + mc) * T:(hh * 2 + mc + 1) * T], mkT[:, mc, h, :],
                                                                              xqT[:, h, 0:T], start=True, stop=True),
                           reads=[b_mkT, b_xqT], writes=[bb])
                xb.append((bk, bb))
                yield
            for hp in range(2):
                bk, bb = xb[hp]
                op("act", lambda e, bk=bk, hp=hp: e.activation(out=xP[:, hp * 4:hp * 4 + 4, 0:T],
                                                               in_=bk[:, 0:4 * T].rearrange("p (a t) -> p a t", t=T), func=AF.Exp),
                   reads=[bb], writes=[b_xP[hp]])
                yield
            for hp in range(2):
                bk, bb = nb(pool)
                for hh in range(2):
                    h = hp * 2 + hh
                    for mc in range(2):
                        op("pe", lambda e, bk=bk, hh=hh, h=h, mc=mc, hp=hp: e.matmul(bk[0:T, hh * 130:(hh + 1) * 130], xP[:, hp * 4 + hh * 2 + mc, 0:T],
                                                                                      mvA[:, mc, h, :], start=(mc == 0), stop=(mc == 1)),
                           reads=[b_xP[hp], b_mvA], writes=[bb])
                yield
                bk3 = bk[0:T, 0:260].rearrange("p (h c) -> p h c", c=130)
                op("dve", lambda e, bk3=bk3, hp=hp: e.reciprocal(out=st[0:T, 28 + hp * 2:30 + hp * 2], in_=bk3[:, :, 128]),
                   reads=[bb], writes=[b_st[5]])
                for hh in range(2):
                    h = hp * 2 + hh
                    op("dve", lambda e, bk=bk, hh=hh, h=h: e.scalar_tensor_tensor(
                        out=scrA[0:T, 1536 + h * 128:1536 + (h + 1) * 128], in0=bk[0:T, hh * 130:hh * 130 + 128],
                        scalar=st[0:T, 28 + h:29 + h], in1=sg_xa[0:T, h * 128:(h + 1) * 128], op0=ALU.mult, op1=ALU.mult),
                       reads=[bb, b_st[5], b_sg[2]], writes=[b_mix[2]])
                yield

        def front_gen(T, slot, next_load, delay):
            for _ in range(delay):
                yield
            xx, bx = xt[slot], b_xt[slot]
            rms_rstd(xx[0:T, :], bx, T, 0, 1.0 / D)
            yield
            op("act", lambda e: e.activation(out=scrA[0:T, 0:D], in_=xx[0:T, :], func=AF.Copy, scale=st[0:T, 0:1]),
               reads=[bx, b_st[0]], writes=[b_scrA, b_mix[0]])
            yield
            bk, bb = nb(BankPool([4]))
            bkb = bk[:].bitcast(BF16)
            for kc in range(8):
                op("pe", lambda e, kc=kc, bkb=bkb: e.transpose(out=bkb[:, kc * T:(kc + 1) * T], in_=scrA[0:T, kc * 128:(kc + 1) * 128],
                                                               identity=ident[0:T, 0:T]), reads=[b_scrA, b_mix[0], b_const], writes=[bb])
            yield
            op("act", lambda e, bkb=bkb: e.activation(out=hk[:, :, 0:T], in_=bkb[:, 0:8 * T].rearrange("p (k t) -> p k t", t=T), func=AF.Copy),
               reads=[bb], writes=[b_hk])
            if next_load is not None:
                next_load()
            yield

        def outproj_gen(T, slot, y_dram_rows):
            xx, bx = xt[slot], b_xt[slot]
            pool = BankPool([0, 1, 2, 3])
            for hf in range(2):
                bk, bb = nb(pool)
                bkb = bk[:].bitcast(BF16)
                for ff in range(8):
                    f = hf * 8 + ff
                    op("pe", lambda e, bkb=bkb, ff=ff, f=f: e.transpose(out=bkb[:, ff * T:(ff + 1) * T], in_=scrA[0:T, f * 128:(f + 1) * 128],
                                                                        identity=ident[0:T, 0:T]), reads=b_mix + [b_const], writes=[bb])
                if hf == 0:
                    op("act", lambda e, bkb=bkb, hf=hf: e.activation(out=qk[:, hf * 8:hf * 8 + 8, 0:T],
                                                                     in_=bkb[:, 0:8 * T].rearrange("p (f t) -> p f t", t=T), func=AF.Copy),
                       reads=[bb], writes=[b_qk, b_qk2])
                else:
                    op("dve", lambda e, bkb=bkb, hf=hf: e.tensor_copy(out=qk[:, hf * 8:hf * 8 + 8, 0:T],
                                                                      in_=bkb[:, 0:8 * T].rearrange("p (f t) -> p f t", t=T)),
                       reads=[bb], writes=[b_qk, b_qk2])
                yield
            for nbk in range(2):
                bk, bb = nb(pool)
                for f in range(16):
                    op("pe", lambda e, bk=bk, f=f, nbk=nbk: e.matmul(bk[0:T, :], qk[:, f, 0:T], wout3[:, f, nbk * 512:(nbk + 1) * 512],
                                                                      start=(f == 0), stop=(f == 15)), reads=[b_qk, b_wout[f]], writes=[bb])
                op("dve", lambda e, bk=bk, nbk=nbk: e.tensor_tensor(out=xx[0:T, nbk * 512:(nbk + 1) * 512], in0=bk[0:T, :],
                                                                    in1=xx[0:T, nbk * 512:(nbk + 1) * 512], op=ALU.add),
                   reads=[bb, bx], writes=[bx])
                yield
            sc = st[0:T, 1:2]
            op("act", lambda e: e.activation(out=junk[0:T, 0:D], in_=xx[0:T, :], func=AF.Square, accum_out=sc),
               reads=[bx], writes=[b_jk[0], b_jk[1], b_st[6]])
            op("act", lambda e: e.activation(out=sc, in_=sc, func=AF.Ln, bias=EPS, scale=1.0 / D), reads=[b_st[6]], writes=[b_st[6]])
            op("act", lambda e: e.activation(out=sc, in_=sc, func=AF.Exp, scale=-0.5), reads=[b_st[6]], writes=[b_st[6]])
            yield
            op("dve", lambda e: e.scalar_tensor_tensor(out=ot[0:T, :], in0=xx[0:T, :], scalar=st[0:T, 1:2], in1=gfin_sb[0:T, :],
                                                       op0=ALU.mult, op1=ALU.mult), reads=[bx, b_st[6], b_gfin], writes=b_otl)
            op("sp", lambda e: e.dma_start(out=y_dram_rows, in_=ot[0:T, :]), reads=b_otl)
            yield

        def tile(T, tabi, tabbuf, y_dram_rows, slot, want_bf, nxt=None, extra=None):
            cx = cx0
            run_interleaved(fq_par_gens(T, cx) + [tok_gen(T, cx, tabi, tabbuf, (4, 5, 6, 7))] + ([extra] if extra is not None else []))
            run_interleaved([hgrn_gen(T, want_bf, cx), ret_gen(T, want_bf, cx), xa_gen(T)])
            gens = [outproj_gen(T, slot, y_dram_rows)]
            if nxt is not None:
                gens.append(front_gen(nxt[0], nxt[1], nxt[2], 2))
            run_interleaved(gens)

        op("pool", lambda e: e.memset(S_sb[:], 0.0), writes=b_S)
        op("pool", lambda e: e.memset(R_sb[:], 0.0), writes=[b_R])
        op("pool", lambda e: e.memset(Sbf[:], 0.0), writes=[b_Sbf])
        op("pool", lambda e: e.memset(Rbf[:], 0.0), writes=[b_Rbf])
        op("pool", lambda e: e.memset(rqTm[:], 0.0), writes=[b_rqkT])
        op("pool", lambda e: e.memset(rqdTm[:], 0.0), writes=[b_rqkT])
        op("pool", lambda e: e.memset(mvA[:], 0.0), writes=[b_mvA])
        op("pool", lambda e: e.memset(mvA[:, :, :, 128:129], 1.0), writes=[b_mvA])

        KSTOP = 9
        njobs = len(bg_jobs)
        per = (njobs + max(NPRE, 1) - 1) // max(NPRE, 1)
        ji = 0
        cur = 0
        def run_interleaved(gens):
            gens = list(gens)
            while gens:
                for g in list(gens):
                    try:
                        next(g)
                    except StopIteration:
                        gens.remove(g)

        def pre_f_gen(hf, cx, Bu, bBu, Bg, bBg, pool, hT, bhT):
            T = 128
            n = 512
            bk, bb = fm_group(C_F + hf * 512, T, pool, hT, bhT)
            yield
            op("act", lambda e: e.activation(out=Bu[:, :], in_=bk[:, :], func=AF.Exp, scale=-1.0), reads=[bb], writes=[bBu])
            yield
            for hh in range(4):
                h = hf * 4 + hh
                op("act", lambda e, hh=hh, h=h: e.activation(out=Bg[:, hh * T:(hh + 1) * T], in_=Bu[:, hh * T:(hh + 1) * T], func=AF.Ln,
                                                             bias=1.0, scale=lbc[:, 24 + h:25 + h]),
                   reads=[bBu, b_lbc], writes=[bBg])
            yield
            op("act", lambda e: e.activation(out=Bu[:, :], in_=Bu[:, :], func=AF.Ln, bias=1.0, scale=1.0), reads=[bBu], writes=[bBu])
            yield
            op("dve", lambda e: e.tensor_tensor(out=Bg[:, :], in0=Bg[:, :], in1=Bu[:, :], op=ALU.subtract), reads=[bBg, bBu], writes=[bBg])
            yield
            for hh in range(4):
                op("dve", lambda e, hh=hh: e.tensor_tensor_scan(out=Bu[:, hh * T:(hh + 1) * T], data0=onesf[:, 0:T],
                                                                data1=Bg[:, hh * T:(hh + 1) * T], initial=0.0,
                                                                op0=ALU.mult, op1=ALU.add),
                   reads=[bBg, b_const], writes=[bBu])
                yield
            gc3 = Bu[:, :].rearrange("p (h t) -> p h t", t=T)
            e0 = cx["E"] + hf * 4
            op("act", lambda e: e.activation(out=Et[:, e0:e0 + 4], in_=gc3[:, :, T - 1], func=AF.Exp), reads=[bBu], writes=[cx["bE"]])
            op("act", lambda e: e.activation(out=Bg[:, :], in_=Bu[:, :], func=AF.Exp, scale=-1.0), reads=[bBu], writes=[bBg])
            yield
            em3 = Bg[:, :].rearrange("p (h t) -> p h t", t=T)
            kd = cx["k"][:, hf * 4:hf * 4 + 4, :]
            op("dve", lambda e: e.tensor_tensor(out=kd[:, :, 1:T], in0=em3[:, :, 1:T], in1=em3[:, :, 0:T - 1], op=ALU.subtract),
               reads=[bBg], writes=[cx["bk"]])
            op("dve", lambda e: e.tensor_scalar_add(out=kd[:, :, 0], in0=em3[:, :, 0], scalar1=-1.0), reads=[bBg], writes=[cx["bk"]])
            yield

        def pre_tok_gen(cx, cur_, pool, hT, bhT):
            for half in range(2):
                bk, bb = tm_group(C_I + half * 512, 512, 128, 0, pool, hT, bhT)
                op("act", lambda e, bk=bk, half=half: e.activation(out=cx["v"][:, half * 512:(half + 1) * 512], in_=bk[:, :], func=AF.Copy),
                   reads=[bb], writes=[cx["bv"]])
                yield
            bk, bb = tm_group(C_RK, 256, 128, 256, pool, hT, bhT)
            yield
            rope(bk, bb, 128, 4, 8, 2 * cur_, b_rtab[cur_])
            yield
            make_rkd(128, cx)
            bk, bb = tm_group(C_RV, 512, 128, 0, pool, hT, bhT)
            yield
            op("act", lambda e, bk=bk: e.activation(out=cx["rv"][:, :], in_=bk[:, :], func=AF.Copy), reads=[bb], writes=[cx["brv"]])
            yield
            rope_advance(cur_)
            yield

        def pre_B_gen(want_bf, cx):
            pool = BankPool([4, 5, 6])
            state_hg_T(128, cx, pool)
            yield
            for _ in state_hg_U_g(128, want_bf, cx, True, pool):
                yield
            state_update_rt(128, want_bf, cx, pool)
            yield

        def pre_front_gen(i, next_load, delay, delay2=0):
            for _ in range(delay):
                yield
            slot = i % 2
            rms_rstd(xt[slot][:, :], b_xt[slot], 128, 0, 1.0 / D)
            yield
            op("pool", lambda e: e.tensor_scalar(out=scrA[:, 0:D], in0=xt[slot][:, :], scalar1=st[:, 0:1], scalar2=1.0, op0=ALU.mult, op1=ALU.mult),
               reads=[b_xt[slot], b_st[0]], writes=[b_scrA])
            yield
            for _ in range(delay2):
                yield
            bk, bb = nb(BankPool([7]))
            bkb = bk[:].bitcast(BF16)
            for kc in range(8):
                op("pe", lambda e, kc=kc: e.transpose(out=bkb[:, kc * 128:(kc + 1) * 128], in_=scrA[:, kc * 128:(kc + 1) * 128],
                                                      identity=ident[:, :]), reads=[b_scrA, b_const], writes=[bb])
            yield
            hTd, bhTd = (hk, b_hk) if i % 2 == 0 else (hk2, b_hk2)
            op("act", lambda e: e.activation(out=hTd[:, :, :], in_=bkb[:, :].rearrange("p (k t) -> p k t", t=128), func=AF.Copy),
               reads=[bb], writes=[bhTd])
            if next_load is not None:
                next_load()
            yield

        def pre_A_gens(i, cur_):
            cx = cx0p if i % 2 == 0 else cx1
            hT, bhT = (hk, b_hk) if i % 2 == 0 else (hk2, b_hk2)
            return [pre_f_gen(0, cx, Bt[0], b_Bt[0], Bt[1], b_Bt[1], BankPool([0]), hT, bhT),
                    pre_f_gen(1, cx, Bt[2], b_Bt[2], Bx, b_Bx, BankPool([1]), hT, bhT),
                    pre_tok_gen(cx, cur_, BankPool([2, 3]), hT, bhT)]

        x_extra[0] = b_sx[0:2]
        x_extra[1] = b_sx[2:4]
        load_x(xp_d[0:128, :], 128, 0)
        if NPRE > 0:
            run_interleaved([pre_front_gen(0, (lambda: load_x(xp_d[128:256, :], 128, 1)) if NPRE > 1 else None, 0)])
        for t in range(NPRE):
            gens = pre_A_gens(t, cur)
            cur = 1 - cur
            if t >= 1:
                gens.append(pre_B_gen(False, cx0p if (t - 1) % 2 == 0 else cx1))
            if t + 1 < NPRE:
                nl = (lambda t=t: load_x(xp_d[(t + 2) * 128:(t + 3) * 128, :], 128, t % 2)) if t + 2 < NPRE else None
                gens.insert(0, pre_front_gen(t + 1, nl, 1, 3))
            run_interleaved(gens)
            for _ in range(per):
                if ji < njobs:
                    bg_jobs[ji]()
                    ji += 1
        if NPRE > 0:
            run_interleaved([pre_B_gen(True, cx0p if (NPRE - 1) % 2 == 0 else cx1)])
        while ji < njobs and KSTOP >= 3:
            bg_jobs[ji]()
            ji += 1

        for mc in range(2 if KSTOP >= 4 else 0):
            slot = mc % 2
            op("sp", lambda e, mc=mc, slot=slot: e.dma_start(out=xt[slot][:, :], in_=mem_d[mc * 128:(mc + 1) * 128, :]), writes=[b_xt[slot]])
            norm_transpose(xt[slot][:, :], b_xt[slot], 128)
            bk, bb = nb()
            for h in range(4):
                for kc in range(8):
                    op("pe", lambda e, bk=bk, h=h, kc=kc: e.matmul(bk[:, h * 128:(h + 1) * 128], wmem4[:, 0, kc, h * 128:(h + 1) * 128],
                                                                   hk[:, kc, :], start=(kc == 0), stop=(kc == 7)),
                       reads=[b_hk, b_wmem[0]], writes=[bb])
            op("act", lambda e, bk=bk, mc=mc: e.activation(out=mkT[:, mc, :, :], in_=bk[:, :].rearrange("p (h m) -> p h m", m=128), func=AF.Copy),
               reads=[bb], writes=[b_mkT])
            for w, od in ((0, mkp_d), (1, mvp_d)):
                bk, bb = nb()
                for kc in range(8):
                    op("pe", lambda e, bk=bk, w=w, kc=kc: e.matmul(bk[:, :], hk[:, kc, :], wmem4[:, w, kc, :], start=(kc == 0), stop=(kc == 7)),
                       reads=[b_hk, b_wmem[w]], writes=[bb])
                op("act", lambda e, bk=bk: e.activation(out=ot[:, 0:512], in_=bk[:, :], func=AF.Copy), reads=[bb], writes=b_otl)
                if w == 1:
                    op("dve", lambda e, bk=bk, mc=mc: e.tensor_copy(out=mvA[:, mc, :, 0:128], in_=bk[:, :].rearrange("p (h v) -> p h v", v=128)),
                       reads=[bb], writes=[b_mvA])
                op("sp", lambda e, od=od, mc=mc: e.dma_start(out=od[mc * 128:(mc + 1) * 128, :], in_=ot[:, 0:512]), reads=b_otl)
        def wout_gen():
            cast_engs[:] = ["pool", "dve", "act"]
            for j in wout_jobs:
                j()
                yield
            cast_engs[:] = ["pool"]

        s0slot = (NPRE + NMAIN) % 2
        if KSTOP >= 5:
            load_x(xp_d[NPRE * 128:(NPRE + 1) * 128, :], 128, NPRE % 2)
            nl0 = (lambda: load_x(xp_d[(NPRE + 1) * 128:(NPRE + 2) * 128, :], 128, (NPRE + 1) % 2)) if NMAIN > 1 else \
                  (lambda: load_x(xs_d[0:64, :], 64, s0slot))
            run_interleaved([front_gen(128, NPRE % 2, nl0, 0)])
        for t in range(NMAIN if KSTOP >= 5 else 0):
            tt = NPRE + t
            if t < NMAIN - 2:
                nxt = (128, (tt + 1) % 2, (lambda tt=tt: load_x(xp_d[(tt + 2) * 128:(tt + 3) * 128, :], 128, tt % 2)))
            elif t == NMAIN - 2:
                nxt = (128, (tt + 1) % 2, (lambda: load_x(xs_d[0:64, :], 64, s0slot)))
            else:
                nxt = None
            tile(128, 2 * cur, b_rtab[cur], yp_d[t * 128:(t + 1) * 128, :], tt % 2, True, nxt, wout_gen() if t == 0 else None)
            if t < NMAIN - 1:
                rope_advance(cur)
                cur = 1 - cur
        op("sp", lambda e: e.dma_start(out=shgp_d.rearrange("h d v -> d h v"), in_=S_sb[:]), reads=b_S)
        op("sp", lambda e: e.dma_start(out=srtp_d.rearrange("(p hh) d v -> (hh d) p v", hh=2), in_=R_sb[:]), reads=[b_R])

        def cacheprep_gen(b, stg, bstg):
            stk = stg[:, :].rearrange("p (c n) -> p c n", n=512)
            op("sp", lambda e: e.dma_start(out=stk, in_=ck_d[b].rearrange("(c p) n -> p c n", p=128)), writes=[bstg])
            yield
            op("dve", lambda e: e.tensor_copy(out=junk[:, :], in_=stg[:, :]), reads=[bstg], writes=[b_junk])
            op("sp", lambda e: e.dma_start(out=stk, in_=cv_d[b].rearrange("(c p) n -> p c n", p=128)), writes=[bstg])
            yield
            bk, bb = nb(BankPool([7]))
            bkb = bk[:].bitcast(BF16)
            for mc in range(2):
                for h in range(4):
                    op("pe", lambda e, bkb=bkb, mc=mc, h=h: e.transpose(out=bkb[:, (mc * 4 + h) * 128:(mc * 4 + h + 1) * 128],
                                                                        in_=junk[:, mc * 512 + h * 128:mc * 512 + (h + 1) * 128], identity=ident[:, :]),
                       reads=[b_junk, b_const], writes=[bb])
            yield
            op("act", lambda e, bkb=bkb: e.activation(out=mkT[:].rearrange("p c h m -> p (c h m)"), in_=bkb[:, :], func=AF.Copy),
               reads=[bb], writes=[b_mkT])
            yield
            op("act", lambda e: e.activation(out=mvA[:, :, :, 0:128], in_=stg[:, :].rearrange("p (c h v) -> p c h v", c=2, h=4), func=AF.Copy),
               reads=[bstg], writes=[b_mvA])
            yield

        for b in range(NSAMP if KSTOP >= 6 else 0):
            slot = (s0slot + b) % 2
            op("sp", lambda e, b=b: e.dma_start(out=S_sb[:], in_=sthg_d[b].rearrange("h d v -> d h v")), writes=b_S)
            op("sp", lambda e, b=b: e.dma_start(out=R_sb[:], in_=strt_d[b].rearrange("(p hh) d v -> (hh d) p v", hh=2)), writes=[b_R])
            op("act", lambda e: e.activation(out=Sbf[:], in_=S_sb[:], func=AF.Copy), reads=b_S, writes=[b_Sbf])
            op("dve", lambda e: e.tensor_copy(out=Rbf[:], in_=R_sb[:]), reads=[b_R], writes=[b_Rbf])
            if b == 0:
                run_interleaved([front_gen(64, slot, None, 0)])
            T = 64
            run_interleaved([cacheprep_gen(b, xt[1 - slot], b_xt[1 - slot])] + fq_par_gens(T, cx0) + [tok_gen(T, cx0, 4, b_rtab[2], (4, 5, 6, 7))])
            if b + 1 < NSAMP:
                load_x(xs_d[(b + 1) * 64:(b + 2) * 64, :], 64, 1 - slot)
            run_interleaved([hgrn_gen(T, False, cx0), ret_gen(T, False, cx0), xa_gen(T)])
            gens = [outproj_gen(T, slot, ys_d[b * 64:(b + 1) * 64, :])]
            if b + 1 < NSAMP:
                gens.append(front_gen(64, 1 - slot, None, 2))
            run_interleaved(gens)
            op("sp", lambda e, b=b: e.dma_start(out=shgs_d[b].rearrange("h d v -> d h v"), in_=S_sb[:]), reads=b_S)
            op("sp", lambda e, b=b: e.dma_start(out=srts_d[b].rearrange("(p hh) d v -> (hh d) p v", hh=2), in_=R_sb[:]), reads=[b_R])

        P.finish()
        with nc.Block() as block:
            P.emit(block)
    return nc


_CACHE = {}


def kernel(x_prompt, x_sample, mem_prompt, state_hgrn, state_ret, cache_mem_k, cache_mem_v,
           norm_g, w_in, lb_logits, hg_norm_g, rt_norm_g, mem_norm_g, w_mem_k, w_mem_v, w_out, final_norm_g):
    f = np.float32
    x_prompt = np.asarray(x_prompt, f)
    x_sample = np.asarray(x_sample, f)
    B, SEQ, _ = x_prompt.shape
    DB = x_sample.shape[0]
    assert B == 2 and SEQ % 512 == 0 and DB % NCORES == 0 and x_sample.shape[1] == 64
    L = SEQ // 4
    NMAIN = L // 128
    NPRE = 3 * NMAIN
    NSAMP = DB // NCORES
    key = (NPRE, NMAIN, NSAMP)
    if key not in _CACHE:
        _CACHE[key] = build(*key)
    nc = _CACHE[key]

    def lay(v, k):
        return np.ascontiguousarray(np.asarray(v, f).reshape(k, 128).T)

    ng = lay(norm_g[0], 8)
    mg = lay(mem_norm_g[0], 8)
    gout = np.concatenate([lay(hg_norm_g[0], 8), lay(rt_norm_g[0], 4)], axis=1)
    lbl = np.concatenate([lay(lb_logits[0], 8), lay(lb_logits[1], 8)], axis=1)
    gfin = np.ascontiguousarray(np.broadcast_to(np.asarray(final_norm_g, f)[None, :], (128, D)))
    win = np.ascontiguousarray(np.asarray(w_in[0], f))
    wout = np.ascontiguousarray(np.asarray(w_out[0], f))
    wmk = np.ascontiguousarray(np.asarray(w_mem_k[0], f))
    wmv = np.ascontiguousarray(np.asarray(w_mem_v[0], f))
    in_maps = []
    for c in range(NCORES):
        s, j = c // 4, c % 4
        end = (j + 1) * L
        xp = np.zeros(((NPRE + NMAIN) * 128, D), f)
        xp[(NPRE + NMAIN) * 128 - end:, :] = x_prompt[s, :end, :]
        posb = np.zeros((128, 2), f)
        posb[:, 0] = np.arange(128, dtype=f) + f(end - (NPRE + NMAIN) * 128)
        posb[:, 1] = np.arange(128, dtype=f) + f(PAST_LEN)
        sl = slice(c * NSAMP, (c + 1) * NSAMP)
        in_maps.append({
            "xp": xp,
            "xs": np.ascontiguousarray(x_sample[sl].reshape(NSAMP * 64, D)),
            "mem": np.ascontiguousarray(np.asarray(mem_prompt[s], f)),
            "sthg": np.ascontiguousarray(np.asarray(state_hgrn[0, sl], f)),
            "strt": np.ascontiguousarray(np.asarray(state_ret[0, sl], f)),
            "ck": np.ascontiguousarray(np.asarray(cache_mem_k[0, sl], f).reshape(NSAMP, NMEM, 512)),
            "cv": np.ascontiguousarray(np.asarray(cache_mem_v[0, sl], f).reshape(NSAMP, NMEM, 512)),
            "win": win, "wout": wout, "wmk": wmk, "wmv": wmv,
            "ng": ng, "mg": mg, "gout": gout, "lbl": lbl, "gfin": gfin, "posb": posb,
        })
    res = run_bass_kernel_spmd(nc, in_maps, core_ids=list(range(NCORES))).results

    y_prompt = np.zeros((B, SEQ, D), f)
    y_sample = np.zeros((DB, 64, D), f)
    shg_p = np.zeros((1, B, 8, 128, 128), f)
    srt_p = np.zeros((1, B, 4, 64, 128), f)
    mk_p = np.zeros((1, B, NMEM, 4, 128), f)
    mv_p = np.zeros((1, B, NMEM, 4, 128), f)
    shg_s = np.zeros((1, DB, 8, 128, 128), f)
    srt_s = np.zeros((1, DB, 4, 64, 128), f)
    for c in range(NCORES):
        s, j = c // 4, c % 4
        r = res[c]
        y_prompt[s, j * L:(j + 1) * L] = r["yp"]
        sl = slice(c * NSAMP, (c + 1) * NSAMP)
        y_sample[sl] = r["ys"].reshape(NSAMP, 64, D)
        shg_s[0, sl] = r["shgs"]
        srt_s[0, sl] = r["srts"]
        if j == 3:
            shg_p[0, s] = r["shgp"]
            srt_p[0, s] = r["srtp"]
            mk_p[0, s] = r["mkp"].reshape(NMEM, 4, 128)
            mv_p[0, s] = r["mvp"].reshape(NMEM, 4, 128)
    return (y_prompt, y_sample, shg_p, srt_p, mk_p, mv_p, shg_s, srt_s)
```
